# Optimizing a Trainium2 kernel written in Bass

```python
import math
import jax, jax.numpy as jnp
from jax import lax
import numpy as np

D_MODEL = 1024
BATCH = 16
SEQ = 2048
DEPTH = 1

CHUNK = 64
EPS = 1e-6
S5_WIDTH = 512
S5_GROUP = 16
S5_GROUPS = S5_WIDTH // S5_GROUP
S5_STATE = 64
HG_WIDTH = 512
HG_HEAD_DIM = 128
HG_HEADS = HG_WIDTH // HG_HEAD_DIM
D_FF = 2816
CONV_W = 3
N_IN = S5_WIDTH + 4 * HG_WIDTH + 2 * D_MODEL

kernel_name = "hybrid_s5_hgrn2_convffn_block"


def rmsnorm(x, gain):
    x32 = x.astype(jnp.float32)
    y = x32 * lax.rsqrt(jnp.mean(x32 * x32, axis=-1, keepdims=True) + EPS)
    return (y * gain.astype(jnp.float32)).astype(x.dtype)


def s5_mixer(u, a_re, a_im, log_dt, b_re, b_im, c_re, c_im, d_skip):
    bsz, seq, _ = u.shape
    f32 = jnp.float32
    ug = u.astype(f32).reshape(bsz, seq, S5_GROUPS, S5_GROUP)
    a_re = a_re.astype(f32); a_im = a_im.astype(f32)
    dt = jnp.exp(log_dt.astype(f32))[:, None]
    mag = jnp.exp(a_re * dt)
    ang = a_im * dt
    lb_re = mag * jnp.cos(ang)
    lb_im = mag * jnp.sin(ang)
    den = a_re * a_re + a_im * a_im
    n_re = lb_re - 1.0
    n_im = lb_im
    co_re = ((n_re * a_re + n_im * a_im) / den)[..., None]
    co_im = ((n_im * a_re - n_re * a_im) / den)[..., None]
    b_re = b_re.astype(f32); b_im = b_im.astype(f32)
    bb_re = co_re * b_re - co_im * b_im
    bb_im = co_re * b_im + co_im * b_re
    bu_re = jnp.einsum('bsgh,gph->bsgp', ug, bb_re)
    bu_im = jnp.einsum('bsgh,gph->bsgp', ug, bb_im)
    al_re = jnp.broadcast_to(lb_re, (1, seq, S5_GROUPS, S5_STATE))
    al_im = jnp.broadcast_to(lb_im, (1, seq, S5_GROUPS, S5_STATE))

    def combine(left, right):
        ar1, ai1, br1, bi1 = left
        ar2, ai2, br2, bi2 = right
        return (ar2 * ar1 - ai2 * ai1,
                ar2 * ai1 + ai2 * ar1,
                ar2 * br1 - ai2 * bi1 + br2,
                ar2 * bi1 + ai2 * br1 + bi2)

    _, _, x_re, x_im = lax.associative_scan(combine, (al_re, al_im, bu_re, bu_im), axis=1)
    y = (jnp.einsum('bsgp,ghp->bsgh', x_re, c_re.astype(f32))
         - jnp.einsum('bsgp,ghp->bsgh', x_im, c_im.astype(f32))
         + d_skip.astype(f32).reshape(S5_GROUPS, S5_GROUP) * ug)
    return y.reshape(bsz, seq, S5_WIDTH).astype(u.dtype)


def hgrn2_mixer(q, f_raw, i, g, lb, norm_gain):
    bsz, seq, _ = q.shape
    nc = seq // CHUNK
    f32 = jnp.float32
    lb = lb.astype(f32).reshape(HG_HEADS, HG_HEAD_DIM)
    f = lb + (1.0 - lb) * jax.nn.sigmoid(f_raw.astype(f32).reshape(bsz, seq, HG_HEADS, HG_HEAD_DIM))
    log_f = jnp.log(f)
    k = 1.0 - f
    qa = jax.nn.silu(q.astype(f32)) * (HG_HEAD_DIM ** -0.5)

    def blocks(t):
        return t.astype(f32).reshape(bsz, nc, CHUNK, HG_HEADS, HG_HEAD_DIM).transpose(1, 0, 2, 3, 4)

    causal = jnp.tril(jnp.ones((CHUNK, CHUNK), dtype=bool))[None, :, :, None, None]

    def step(state, xs):
        qc, kc, ic, lfc = xs
        bcum = jnp.cumsum(lfc, axis=1)
        o_inter = jnp.einsum('bthk,bhkv->bthv', qc * jnp.exp(bcum), state)
        decay = jnp.exp(jnp.where(causal, bcum[:, :, None] - bcum[:, None, :], -jnp.inf))
        scores = jnp.sum(qc[:, :, None] * kc[:, None, :] * decay, axis=-1)
        o_intra = jnp.einsum('btsh,bshv->bthv', scores, ic)
        b_last = bcum[:, -1]
        state = (jnp.exp(b_last)[..., None] * state
                 + jnp.einsum('bshk,bshv->bhkv', kc * jnp.exp(b_last[:, None] - bcum), ic))
        return state, o_inter + o_intra

    state0 = jnp.zeros((bsz, HG_HEADS, HG_HEAD_DIM, HG_HEAD_DIM), f32)
    _, o = lax.scan(step, state0, (blocks(qa), blocks(k), blocks(i), blocks(log_f)))
    o = o.transpose(1, 0, 2, 3, 4).reshape(bsz, seq, HG_HEADS, HG_HEAD_DIM)
    o = rmsnorm(o, norm_gain.reshape(HG_HEADS, HG_HEAD_DIM))
    o = o.reshape(bsz, seq, HG_WIDTH) * jax.nn.silu(g.astype(f32))
    return o.astype(q.dtype)


def conv_ffn(u, w_up, w_conv, b_conv, w_down):
    seq = u.shape[1]
    h = u @ w_up
    hp = jnp.pad(h, ((0, 0), (CONV_W - 1, 0), (0, 0)))
    hc = hp[:, 0:seq] * w_conv[0]
    for j in range(1, CONV_W):
        hc = hc + hp[:, j:j + seq] * w_conv[j]
    hc = hc + b_conv
    gate, val = jnp.split(hc, 2, axis=-1)
    return (jax.nn.silu(gate) * val) @ w_down


def setup_inputs(seed: int = 0) -> dict:
    key = jax.random.key(seed)
    ks = jax.random.split(key, 24)
    f32 = jnp.float32
    L = DEPTH
    nrm = lambda k, shp, s: jax.random.normal(k, shp, f32) * s
    x = jax.random.normal(ks[0], (BATCH, SEQ, D_MODEL), f32)
    g_mix = 1.0 + nrm(ks[1], (L, D_MODEL), 0.01)
    w_in = nrm(ks[2], (L, D_MODEL, N_IN), D_MODEL ** -0.5)
    s5_a_re = -0.5 * (1.0 + nrm(ks[3], (L, S5_GROUPS, S5_STATE), 0.02))
    s5_a_im = jnp.broadcast_to(jnp.pi * jnp.arange(S5_STATE, dtype=f32), (L, S5_GROUPS, S5_STATE)) + nrm(ks[4], (L, S5_GROUPS, S5_STATE), 0.01)
    s5_log_dt = jax.random.uniform(ks[5], (L, S5_GROUPS), f32, math.log(1e-3), math.log(1e-1))
    s5_b_re = nrm(ks[6], (L, S5_GROUPS, S5_STATE, S5_GROUP), (2 * S5_GROUP) ** -0.5)
    s5_b_im = nrm(ks[7], (L, S5_GROUPS, S5_STATE, S5_GROUP), (2 * S5_GROUP) ** -0.5)
    s5_c_re = nrm(ks[8], (L, S5_GROUPS, S5_GROUP, S5_STATE), (2 * S5_STATE) ** -0.5)
    s5_c_im = nrm(ks[9], (L, S5_GROUPS, S5_GROUP, S5_STATE), (2 * S5_STATE) ** -0.5)
    s5_d = nrm(ks[10], (L, S5_WIDTH), 1.0)
    w_glu = nrm(ks[11], (L, S5_WIDTH, S5_WIDTH), S5_WIDTH ** -0.5)
    b_glu = nrm(ks[12], (L, S5_WIDTH), 0.01)
    hg_lb_logits = nrm(ks[13], (L + 1, HG_WIDTH), 0.1)
    hg_norm_gain = 1.0 + nrm(ks[14], (L, HG_WIDTH), 0.01)
    w_pa = nrm(ks[15], (L, S5_WIDTH, D_MODEL), S5_WIDTH ** -0.5)
    w_pb = nrm(ks[16], (L, HG_WIDTH, D_MODEL), HG_WIDTH ** -0.5)
    w_out = nrm(ks[17], (L, D_MODEL, D_MODEL), D_MODEL ** -0.5)
    g_ffn = 1.0 + nrm(ks[18], (L, D_MODEL), 0.01)
    w_up = nrm(ks[19], (L, D_MODEL, 2 * D_FF), D_MODEL ** -0.5)
    w_conv = nrm(ks[20], (L, CONV_W, 2 * D_FF), CONV_W ** -0.5)
    b_conv = nrm(ks[21], (L, 2 * D_FF), 0.01)
    w_down = nrm(ks[22], (L, D_FF, D_MODEL), D_FF ** -0.5)
    g_final = 1.0 + nrm(ks[23], (D_MODEL,), 0.01)
    return {"x": x, "g_mix": g_mix, "w_in": w_in, "s5_a_re": s5_a_re, "s5_a_im": s5_a_im,
            "s5_log_dt": s5_log_dt, "s5_b_re": s5_b_re, "s5_b_im": s5_b_im, "s5_c_re": s5_c_re,
            "s5_c_im": s5_c_im, "s5_d": s5_d, "w_glu": w_glu, "b_glu": b_glu,
            "hg_lb_logits": hg_lb_logits, "hg_norm_gain": hg_norm_gain, "w_pa": w_pa, "w_pb": w_pb,
            "w_out": w_out, "g_ffn": g_ffn, "w_up": w_up, "w_conv": w_conv, "b_conv": b_conv,
            "w_down": w_down, "g_final": g_final}


def reference(x, g_mix, w_in, s5_a_re, s5_a_im, s5_log_dt, s5_b_re, s5_b_im, s5_c_re, s5_c_im,
              s5_d, w_glu, b_glu, hg_lb_logits, hg_norm_gain, w_pa, w_pb, w_out, g_ffn, w_up,
              w_conv, b_conv, w_down, g_final):
    lb_all = jnp.cumsum(jax.nn.softmax(hg_lb_logits.astype(jnp.float32), axis=0), axis=0)
    splits = [S5_WIDTH + j * HG_WIDTH for j in range(5)] + [S5_WIDTH + 4 * HG_WIDTH + D_MODEL]
    for l in range(DEPTH):
        u = rmsnorm(x, g_mix[l])
        z = u @ w_in[l]
        za, zq, zf, zi, zg, zga, zgb = jnp.split(z, splits, axis=-1)
        ya = s5_mixer(za, s5_a_re[l], s5_a_im[l], s5_log_dt[l], s5_b_re[l], s5_b_im[l],
                      s5_c_re[l], s5_c_im[l], s5_d[l])
        ya = jax.nn.gelu(ya)
        ya = ya * jax.nn.sigmoid(ya @ w_glu[l] + b_glu[l])
        yb = hgrn2_mixer(zq, zf, zi, zg, lb_all[l], hg_norm_gain[l])
        m = jax.nn.sigmoid(zga) * (ya @ w_pa[l]) + jax.nn.sigmoid(zgb) * (yb @ w_pb[l])
        x = x + m @ w_out[l]
        x = x + conv_ffn(rmsnorm(x, g_ffn[l]), w_up[l], w_conv[l], b_conv[l], w_down[l])
    return rmsnorm(x, g_final)
```

```python
import math
import threading
from contextlib import ExitStack

import numpy as np
import concourse.bass as bass
import concourse.mybir as mybir
from concourse.bass_utils import run_bass_kernel_spmd

F32 = mybir.dt.float32
BF16 = mybir.dt.bfloat16
I32 = mybir.dt.int32
AF = mybir.ActivationFunctionType
ALU = mybir.AluOpType

ENGS = ["tensor", "vector", "scalar", "gpsimd", "sync"]
RING = 8
NCORES = 8
SEQ = 2048
DM = 1024
NT = 512
NBLK = SEQ // NT
DFF = 2816
TWO_PI = 2.0 * math.pi
TWO_PI_HI = 6.2831854820251465
TWO_PI_LO = -1.7484556025237907e-07

GM, GF, BG, SD, HG, L0, L1, WC0, WC1, WC2, BC, NPV = 0, 8, 16, 20, 24, 28, 32, 36, 80, 124, 168, 212


class Prog:
    def __init__(self, nc, ctx):
        self.nc = nc
        self.q = {e: [] for e in ENGS}
        self.sem = {e: ctx.enter_context(nc.semaphore("s_" + e)) for e in ENGS}
        self.ring = {
            e: [ctx.enter_context(nc.semaphore("d_%s_%d" % (e, i))) for i in range(RING)]
            for e in ("sync", "gpsimd")
        }
        self.ring_cnt = {e: [0] * RING for e in ("sync", "gpsimd")}
        self.ring_pos = {e: 0 for e in ("sync", "gpsimd")}
        self.lastw = {}
        self.readers = {}
        self.final_events = []

    def _deps(self, eng, reads, writes):
        deps = set()
        for r in reads:
            ev = self.lastw.get(r)
            if ev is not None:
                deps.add(ev)
        for w in writes:
            ev = self.lastw.get(w)
            if ev is not None:
                deps.add(ev)
            for ev in self.readers.get(w, ()):
                deps.add(ev)
        return deps

    def _commit(self, ev, reads, writes):
        for r in reads:
            self.readers.setdefault(r, []).append(ev)
        for w in writes:
            self.lastw[w] = ev
            self.readers[w] = []

    def op(self, eng, fn, reads=(), writes=(), sig=True):
        deps = self._deps(eng, reads, writes)
        ev = ("e", eng, len(self.q[eng]))
        self.q[eng].append(dict(deps=deps, fn=fn, sig=sig, kind="op"))
        self._commit(ev, reads, writes)
        if Interleave.current is not None:
            Interleave.current.switch()
        return ev

    def dma(self, eng, out, in_, reads=(), writes=(), final=False):
        deps = self._deps(eng, reads, writes)
        s = self.ring_pos[eng]
        self.ring_pos[eng] = (s + 1) % RING
        sem = self.ring[eng][s]
        prev = self.ring_cnt[eng][s]
        if prev > 0:
            deps.add(("d", sem, prev))
        self.ring_cnt[eng][s] = prev + 16
        ev = ("d", sem, prev + 16)
        self.q[eng].append(dict(deps=deps, out=out, in_=in_, sem=sem, kind="dma"))
        self._commit(ev, reads, writes)
        if final:
            self.final_events.append(ev)
        return ev

    def finalize(self):
        nc = self.nc
        self.q["sync"].append(dict(deps=set(self.final_events), kind="nop"))
        sigcount = {}
        for e in ENGS:
            for ins in reversed(self.q[e]):
                if ins["kind"] == "op":
                    ins["sig"] = True
                    break
            c = 0
            arr = []
            for ins in self.q[e]:
                if ins["kind"] == "op" and ins["sig"]:
                    c += 1
                arr.append(c)
            sigcount[e] = arr

        def resolve(ev):
            if ev[0] == "d":
                return (ev[1], ev[2])
            _, eng, idx = ev
            q = self.q[eng]
            j = idx
            while not (q[j]["kind"] == "op" and q[j]["sig"]):
                j += 1
            return (self.sem[eng], sigcount[eng][j])

        with nc.Block() as block:
            for e in ENGS:
                def body(engine, e=e):
                    seen = {}
                    for ins in self.q[e]:
                        for ev in ins["deps"]:
                            if ev[0] == "e" and ev[1] == e and e == "tensor":
                                continue
                            sem, val = resolve(ev)
                            k = id(sem)
                            if seen.get(k, 0) >= val:
                                continue
                            seen[k] = val
                            engine.wait_ge(sem, val)
                        if ins["kind"] == "op":
                            r = ins["fn"](engine)
                            if ins["sig"]:
                                r.then_inc(self.sem[e], 1)
                        elif ins["kind"] == "dma":
                            engine.dma_start(out=ins["out"], in_=ins["in_"]).then_inc(ins["sem"], 16)

                getattr(block, e)(body)


class Interleave:
    current = None

    def __init__(self, fns):
        self.fns = fns
        self.n = len(fns)
        self.turn = 0
        self.alive = [True] * self.n
        self.cv = threading.Condition()
        self.err = None
        self.tl = threading.local()

    def _next(self, i):
        for d in range(1, self.n + 1):
            j = (i + d) % self.n
            if self.alive[j]:
                return j
        return -1

    def _wrap(self, i):
        with self.cv:
            while self.turn != i:
                self.cv.wait()
        self.tl.me = i
        try:
            self.fns[i]()
        except BaseException as ex:
            self.err = ex
        finally:
            with self.cv:
                self.alive[i] = False
                self.turn = self._next(i)
                self.cv.notify_all()

    def switch(self):
        i = getattr(self.tl, "me", None)
        if i is None or getattr(self.tl, "atomic", 0) > 0:
            return
        with self.cv:
            nxt = self._next(i)
            if nxt == i or nxt < 0:
                return
            self.turn = nxt
            self.cv.notify_all()
            while self.turn != i:
                self.cv.wait()

    def atomic(self, on):
        self.tl.atomic = getattr(self.tl, "atomic", 0) + (1 if on else -1)

    def run(self):
        Interleave.current = self
        ths = [threading.Thread(target=self._wrap, args=(i,)) for i in range(self.n)]
        for t in ths:
            t.start()
        for t in ths:
            t.join()
        Interleave.current = None
        if self.err is not None:
            raise self.err


def C(name, *a, **k):
    return lambda e: getattr(e, name)(*a, **k)


class Buf:
    def __init__(self, arena, off_kb, shape, dt):
        self.shape = shape
        self.dt = dt
        esz = 4 if dt in (F32, I32) else 2
        n = int(np.prod(shape[1:]))
        self.nbytes = n * esz
        self.off = off_kb * 1024
        a = arena[:, self.off // 2:(self.off + self.nbytes) // 2]
        if esz == 4:
            a = a.bitcast(dt)
        if len(shape) == 3:
            a = a.rearrange("p (a b) -> p a b", b=shape[2])
        elif len(shape) == 4:
            a = a.rearrange("p (a b c) -> p a b c", b=shape[2], c=shape[3])
        self.a = a
        self.sub = (self.nbytes // shape[1]) if len(shape) >= 3 else self.nbytes

    def k(self, i=None, n=1):
        if i is None:
            lo, hi = self.off, self.off + self.nbytes
        else:
            lo = self.off + i * self.sub
            hi = lo + n * self.sub
        return [("ar", g) for g in range(lo // 1024, (hi + 1023) // 1024)]


def build_program(nblk=NBLK, nseq=2, dbg=False):
    nc = bass.Bass("TRN2", target_bir_lowering=False)
    dram = lambda name, shape, kind="ExternalInput": nc.dram_tensor(name, shape, F32, kind=kind).ap()
    x_d = dram("x", [2, SEQ, DM])
    w_in_d = dram("w_in", [DM, 4608])
    w_glu_d = dram("w_glu", [512, 512])
    w_pa_d = dram("w_pa", [512, DM])
    w_pb_d = dram("w_pb", [512, DM])
    w_out_d = dram("w_out", [DM, DM])
    w_up_d = dram("w_up", [DM, 2 * DFF])
    w_down_d = dram("w_down", [DFF, DM])
    pv_d = dram("pv", [128, NPV])
    gfin_d = dram("gfin", [128, DM])
    s5s_d = dram("s5s", [128, 3, 16])
    s5b_d = dram("s5b", [128, 3, 512])
    s5bb_d = dram("s5bb", [128, 2, 512])
    s5cc_d = dram("s5cc", [128, 2, 512])
    cst_d = dram("cst", [128, 128 + 3 * 512 + 64 + 8])
    out_d = dram("out", [2, SEQ, DM], kind="ExternalOutput")
    NG = 31
    ws_d = nc.dram_tensor("wscratch", [NG, 128, 8, 512], BF16, kind="Internal").ap()
    if dbg:
        dbg_d = dram("dbg", [8, 128, 512], kind="ExternalOutput")

    ctx = ExitStack()
    with ctx:
        P = Prog(nc, ctx)
        sbt = lambda name, shape, dt: ctx.enter_context(nc.sbuf_tensor("sb_" + name, shape, dt))
        pst = lambda name, shape, dt: ctx.enter_context(nc.psum_tensor("ps_" + name, shape, dt))

        def atomic(on):
            if Interleave.current is not None:
                Interleave.current.atomic(on)

        def V(fn, r=(), w=()):
            return P.op("vector", fn, r, w)

        def G(fn, r=(), w=()):
            return P.op("gpsimd", fn, r, w)

        def A(fn, r=(), w=()):
            return P.op("scalar", fn, r, w)

        def T(fn, r=(), w=(), sig=True):
            return P.op("tensor", fn, r, w, sig)

        pv = sbt("pv", [128, NPV], F32)
        gfin = sbt("gfin", [128, DM], F32)
        identb = sbt("identb", [128, 128], BF16)
        onesb = sbt("onesb", [128, 128], BF16)
        cst = sbt("cst", [128, 3 * 512 + 64 + 8], F32)
        maskblk = cst[:, 0:512]
        mask64 = cst[:, 512:1024]
        mask8 = cst[:, 1024:1536]
        avals = cst[:, 1536:1600]
        bvals = cst[:, 1600:1608]
        lbv = sbt("lbv", [128, 12], F32)
        Bm = sbt("Bm", [128, 4, 8, 2, 128], BF16)
        Cm = sbt("Cm", [128, 16, 8, 2, 32], BF16)
        T1 = sbt("T1", [128, 2, 16, 64], BF16)
        T2 = sbt("T2", [128, 2, 16, 64], BF16)
        L7 = sbt("L7", [128, 2, 16], F32)
        NWB = 4
        wbuf = [sbt("wbuf%d" % i, [128, 8, 512], BF16) for i in range(NWB)]
        Sf = sbt("Sf", [128, 4, 128], F32)
        SbP = sbt("SbP", [128, 4, 128], BF16)
        Sb8 = sbt("Sb8", [128, 8, 128], BF16)
        LS = sbt("LS", [128, 16, 2, 65], F32)
        muZ = sbt("muZ", [128, 16, 2], F32)
        mzt = sbt("mzt", [128, 4, 16], F32)
        hprev = sbt("hprev", [128, 44, 2], BF16)
        stat = sbt("stat", [128, 16], F32)

        ARENA_KB = 110
        arena = sbt("arena", [128, ARENA_KB * 512], BF16)
        B_ = lambda off, shape, dt: Buf(arena[:], off, shape, dt)
        xt = B_(0, [128, 4, DM], F32)
        uT = B_(16, [128, 8, NT], BF16)
        xs = B_(24, [128, 4, DM], BF16)
        zaT = B_(24, [128, 4, NT], BF16)
        zi = B_(28, [128, 4, NT], BF16)
        qa = B_(32, [128, 4, NT], BF16)
        sf = B_(36, [128, 4, NT], BF16)
        sg = B_(40, [128, 4, NT], BF16)
        Vsb = [B_(88, [128, 4, 2, NT], BF16), B_(96, [128, 4, 2, NT], BF16)]
        mT = B_(32, [128, 8, NT], BF16)
        aT = B_(32, [128, 22, NT], BF16)
        logf = B_(48, [128, NT], F32)
        bcum = B_(50, [128, NT], F32)
        ebts = [B_(52, [128, NT], F32), B_(104, [128, NT], F32)]
        enb = B_(54, [128, NT], F32)
        KhT = B_(56, [128, NT], BF16)
        Khs = [B_(57, [128, 4, 128], BF16), B_(106, [128, 4, 128], BF16)]
        sTs = [B_(58, [128, NT], BF16), B_(107, [128, NT], BF16)]
        osq = B_(59, [128, NT], BF16)
        hrs = B_(60, [128, NT], F32)
        ytm = B_(62, [128, NT], F32)
        l2 = [B_(o_, [128, 4, 64], F32) for o_ in (44, 45, 46, 47, 73, 74)]
        l2x = [B_(o_, [128, 4, 64], F32) for o_ in (108, 109)]
        ybT = B_(65, [128, 4, NT], BF16)
        LSb = B_(69, [128, 4, 2, 64], BF16)
        yv = B_(71, [128, NT], F32)
        y2 = B_(73, [128, NT], F32)
        yaP = B_(75, [128, 4, NT], BF16)
        yaT = B_(79, [128, 4, NT], BF16)
        gsig = B_(83, [128, NT], BF16)
        siga = B_(84, [128, NT], BF16)
        sigb = B_(85, [128, NT], BF16)
        t1 = B_(86, [128, NT], F32)
        hs = [[B_(54 + 4 * p_ + 2 * i, [128, 1024], BF16) for i in range(2)] for p_ in range(2)]
        cc = [[B_(62 + 4 * p_ + 2 * i, [128, NT], F32) for i in range(2)] for p_ in range(2)]
        gact = [B_(70 + p_, [128, NT], BF16) for p_ in range(2)]

        NPB = 8
        pb = [pst("pb%d" % i, [128, 512], F32) for i in range(NPB)]
        pbk = lambda i: "pb%d" % i
        pbh = [pb[i][:].bitcast(BF16) for i in range(NPB)]
        bank_rr = [0]

        held = set()

        def nbank(hold=False):
            tries = 0
            while True:
                b = bank_rr[0]
                bank_rr[0] = (b + 1) % NPB
                if b not in held:
                    break
                tries += 1
                if tries % NPB == 0:
                    assert Interleave.current is not None and tries < 100000, "PSUM banks exhausted"
                    Interleave.current.switch()
            if hold:
                held.add(b)
            return b

        P.dma("sync", pv[:], pv_d, writes=["pv"])
        P.dma("sync", gfin[:], gfin_d, writes=["gfin"])
        P.dma("sync", cst[:], cst_d[:, 128:], writes=["cst"])
        P.dma("gpsimd", identb[:], cst_d[:, 0:128], writes=["identb"])
        V(C("memset", onesb[:], 1.0), w=["onesb"])

        V(C("tensor_tensor", out=lbv[:, 8:12], in0=pv[:, L0:L0 + 4], in1=pv[:, L1:L1 + 4], op=ALU.subtract), r=["pv"], w=["lbv"])
        A(C("activation", out=lbv[:, 0:4], in_=lbv[:, 8:12], func=AF.Sigmoid), r=["lbv"], w=["lbv"])
        A(C("activation", out=lbv[:, 4:8], in_=lbv[:, 8:12], func=AF.Sigmoid, scale=-1.0), r=["lbv"], w=["lbv"])
        V(C("tensor_scalar", out=lbv[:, 8:12], in0=lbv[:, 4:8], scalar1=-1.0, scalar2=None, op0=ALU.mult), r=["lbv"], w=["lbv"])

        prep_id = [0]

        def s5_prep():
            tmp = [B_(4 * i, [128, 1024], F32) for i in range(14)]
            tmi = B_(56, [128, 1024], I32)
            s5s = B_(60, [128, 3, 16], F32)
            s5b = B_(61, [128, 3, 512], F32)
            s5bb = B_(67, [128, 2, 512], F32)
            s5cc = B_(71, [128, 2, 512], F32)
            P.dma("sync", s5s.a, s5s_d, writes=s5s.k())
            P.dma("sync", s5b.a, s5b_d, writes=s5b.k())
            P.dma("sync", s5bb.a, s5bb_d, writes=s5bb.k())
            P.dma("sync", s5cc.a, s5cc_d, writes=s5cc.k())

            def sin_rr(out, xin, n, shift, tA, tB):
                ta, tb = tA.a[:, 0:n], tB.a[:, 0:n]
                ti = tmi.a[:, 0:n]
                rk = out["k"] + xin["k"] + tA.k() + tB.k() + tmi.k()
                V(C("tensor_scalar", out=ta, in0=xin["a"], scalar1=shift, scalar2=None, op0=ALU.add), r=rk, w=tA.k())
                V(C("tensor_scalar", out=ti, in0=ta, scalar1=1.0 / TWO_PI, scalar2=None, op0=ALU.mult), r=rk, w=tmi.k())
                V(C("tensor_copy", out=tb, in_=ti), r=rk, w=tB.k())
                V(C("scalar_tensor_tensor", out=ta, in0=tb, scalar=-TWO_PI_HI, in1=ta, op0=ALU.mult, op1=ALU.add), r=rk, w=tA.k())
                V(C("scalar_tensor_tensor", out=ta, in0=tb, scalar=-TWO_PI_LO, in1=ta, op0=ALU.mult, op1=ALU.add), r=rk, w=tA.k())
                V(C("tensor_scalar", out=tb, in0=ta, scalar1=math.pi, scalar2=None, op0=ALU.is_gt), r=rk, w=tB.k())
                V(C("scalar_tensor_tensor", out=ta, in0=tb, scalar=-TWO_PI_HI, in1=ta, op0=ALU.mult, op1=ALU.add), r=rk, w=tA.k())
                V(C("scalar_tensor_tensor", out=ta, in0=tb, scalar=-TWO_PI_LO, in1=ta, op0=ALU.mult, op1=ALU.add), r=rk, w=tA.k())
                V(C("tensor_scalar", out=tb, in0=ta, scalar1=-math.pi, scalar2=None, op0=ALU.is_lt), r=rk, w=tB.k())
                V(C("scalar_tensor_tensor", out=ta, in0=tb, scalar=TWO_PI_HI, in1=ta, op0=ALU.mult, op1=ALU.add), r=rk, w=tA.k())
                V(C("scalar_tensor_tensor", out=ta, in0=tb, scalar=TWO_PI_LO, in1=ta, op0=ALU.mult, op1=ALU.add), r=rk, w=tA.k())
                V(C("tensor_scalar", out=ta, in0=ta, scalar1=math.pi, scalar2=-math.pi, op0=ALU.min, op1=ALU.max), r=rk, w=tA.k())
                A(C("activation", out=out["a"], in_=ta, func=AF.Sin), r=rk, w=out["k"])

            def cpow(ore, oim, ardt, ang, expo, n, tA, tB, tC, tD):
                xa = dict(a=tC.a[:, 0:n], k=tC.k())
                mg = dict(a=tD.a[:, 0:n], k=tD.k())
                rk = ardt["k"] + ang["k"] + tC.k() + tD.k() + ore["k"] + oim["k"]
                if isinstance(expo, float):
                    V(C("tensor_scalar", out=xa["a"], in0=ang["a"], scalar1=expo, scalar2=None, op0=ALU.mult), r=rk, w=tC.k())
                    A(C("activation", out=mg["a"], in_=ardt["a"], func=AF.Exp, scale=expo), r=rk, w=tD.k())
                else:
                    rk = rk + expo["k"]
                    V(C("tensor_tensor", out=xa["a"], in0=ang["a"], in1=expo["a"], op=ALU.mult), r=rk, w=tC.k())
                    V(C("tensor_tensor", out=mg["a"], in0=ardt["a"], in1=expo["a"], op=ALU.mult), r=rk, w=tD.k())
                    A(C("activation", out=mg["a"], in_=mg["a"], func=AF.Exp), r=rk, w=tD.k())
                sin_rr(oim, xa, n, 0.0, tA, tB)
                sin_rr(ore, xa, n, 0.5 * math.pi, tA, tB)
                V(C("tensor_tensor", out=ore["a"], in0=ore["a"], in1=mg["a"], op=ALU.mult), r=rk, w=ore["k"])
                V(C("tensor_tensor", out=oim["a"], in0=oim["a"], in1=mg["a"], op=ALU.mult), r=rk, w=oim["k"])

            def base(src, n, t_dt, t_ardt, t_ang):
                A(C("activation", out=t_dt.a[:, 0:n], in_=src.a[:, 2], func=AF.Exp), r=src.k(), w=t_dt.k())
                V(C("tensor_tensor", out=t_ardt.a[:, 0:n], in0=src.a[:, 0], in1=t_dt.a[:, 0:n], op=ALU.mult), r=src.k() + t_dt.k(), w=t_ardt.k())
                V(C("tensor_tensor", out=t_ang.a[:, 0:n], in0=src.a[:, 1], in1=t_dt.a[:, 0:n], op=ALU.mult), r=src.k() + t_dt.k(), w=t_ang.k())

            D_ = lambda t, n: dict(a=t.a[:, 0:n], k=t.k())

            n = 512
            t_dt, t_ardt, t_ang = tmp[0], tmp[1], tmp[2]
            base(s5b, n, t_dt, t_ardt, t_ang)
            ardt, ang = D_(t_ardt, n), D_(t_ang, n)
            lr, li = D_(tmp[3], n), D_(tmp[4], n)
            cpow(lr, li, ardt, ang, 1.0, n, tmp[5], tmp[6], tmp[7], tmp[8])
            cor, coi = D_(tmp[9], n), D_(tmp[10], n)
            are, aim = s5b.a[:, 0], s5b.a[:, 1]
            den, nre = tmp[5].a[:, 0:n], tmp[6].a[:, 0:n]
            q1, q2 = tmp[7].a[:, 0:n], tmp[8].a[:, 0:n]
            allk = sum([t.k() for t in tmp], []) + s5b.k() + s5bb.k() + s5cc.k() + s5s.k()
            V(C("tensor_tensor", out=den, in0=are, in1=are, op=ALU.mult), r=allk, w=tmp[5].k())
            V(C("tensor_tensor", out=q1, in0=aim, in1=aim, op=ALU.mult), r=allk, w=tmp[7].k())
            V(C("tensor_tensor", out=den, in0=den, in1=q1, op=ALU.add), r=allk, w=tmp[5].k())
            V(C("reciprocal", out=den, in_=den), r=allk, w=tmp[5].k())
            V(C("tensor_scalar", out=nre, in0=lr["a"], scalar1=-1.0, scalar2=None, op0=ALU.add), r=allk, w=tmp[6].k())
            V(C("tensor_tensor", out=q1, in0=nre, in1=are, op=ALU.mult), r=allk, w=tmp[7].k())
            V(C("tensor_tensor", out=q2, in0=li["a"], in1=aim, op=ALU.mult), r=allk, w=tmp[8].k())
            V(C("tensor_tensor", out=q1, in0=q1, in1=q2, op=ALU.add), r=allk, w=tmp[7].k())
            V(C("tensor_tensor", out=cor["a"], in0=q1, in1=den, op=ALU.mult), r=allk, w=tmp[9].k())
            V(C("tensor_tensor", out=q1, in0=li["a"], in1=are, op=ALU.mult), r=allk, w=tmp[7].k())
            V(C("tensor_tensor", out=q2, in0=nre, in1=aim, op=ALU.mult), r=allk, w=tmp[8].k())
            V(C("tensor_tensor", out=q1, in0=q1, in1=q2, op=ALU.subtract), r=allk, w=tmp[7].k())
            V(C("tensor_tensor", out=coi["a"], in0=q1, in1=den, op=ALU.mult), r=allk, w=tmp[10].k())
            bre = s5bb.a[:, 0]
            bim = s5bb.a[:, 1]
            ir_, ii_ = D_(tmp[3], n), D_(tmp[4], n)
            cpow(ir_, ii_, ardt, ang, -1.0, n, tmp[5], tmp[6], tmp[7], tmp[8])
            pairs = [(tmp[9], tmp[10]), (tmp[11], tmp[12])]
            for b in range(8):
                u1, u2 = tmp[5].a[:, 0:n], tmp[6].a[:, 0:n]
                if b > 0:
                    (pcr, pci), (ncr, nci) = pairs[(b - 1) % 2], pairs[b % 2]
                    V(C("tensor_tensor", out=u1, in0=pcr.a[:, 0:n], in1=ir_["a"], op=ALU.mult), r=allk, w=tmp[5].k())
                    V(C("tensor_tensor", out=u2, in0=pci.a[:, 0:n], in1=ii_["a"], op=ALU.mult), r=allk, w=tmp[6].k())
                    V(C("tensor_tensor", out=ncr.a[:, 0:n], in0=u1, in1=u2, op=ALU.subtract), r=allk, w=ncr.k())
                    V(C("tensor_tensor", out=u1, in0=pcr.a[:, 0:n], in1=ii_["a"], op=ALU.mult), r=allk, w=tmp[5].k())
                    V(C("tensor_tensor", out=u2, in0=pci.a[:, 0:n], in1=ir_["a"], op=ALU.mult), r=allk, w=tmp[6].k())
                    V(C("tensor_tensor", out=nci.a[:, 0:n], in0=u1, in1=u2, op=ALU.add), r=allk, w=nci.k())
                cr, ci = pairs[b % 2][0].a[:, 0:n], pairs[b % 2][1].a[:, 0:n]
                o_re = Bm[:, :, b, 0, :]
                o_im = Bm[:, :, b, 1, :]
                v4 = lambda ap: ap.rearrange("p (a c) -> p a c", c=128)
                V(C("tensor_tensor", out=u1, in0=cr, in1=bre, op=ALU.mult), r=allk, w=tmp[5].k())
                V(C("tensor_tensor", out=u2, in0=ci, in1=bim, op=ALU.mult), r=allk, w=tmp[6].k())
                V(C("tensor_tensor", out=o_re, in0=v4(u1), in1=v4(u2), op=ALU.subtract), r=allk, w=["Bm"])
                V(C("tensor_tensor", out=u1, in0=cr, in1=bim, op=ALU.mult), r=allk, w=tmp[5].k())
                V(C("tensor_tensor", out=u2, in0=ci, in1=bre, op=ALU.mult), r=allk, w=tmp[6].k())
                V(C("tensor_tensor", out=o_im, in0=v4(u1), in1=v4(u2), op=ALU.add), r=allk, w=["Bm"])

            t_dt, t_ardt, t_ang = tmp[0], tmp[1], tmp[2]
            base(s5s, 16, t_dt, t_ardt, t_ang)
            ardt16, ang16 = t_ardt.a[:, 0:16], t_ang.a[:, 0:16]
            n = 128
            ex = D_(tmp[3], n)
            ar_b, an_b = D_(tmp[4], n), D_(tmp[5], n)
            b3 = lambda ap, m: ap.rearrange("p (j a) -> p j a", a=m)
            V(C("tensor_copy", out=b3(ex["a"], 8), in_=bvals.unsqueeze(1).to_broadcast([128, 16, 8])), r=allk + ["cst"], w=tmp[3].k())
            V(C("tensor_copy", out=b3(ar_b["a"], 8), in_=ardt16.unsqueeze(2).to_broadcast([128, 16, 8])), r=allk, w=tmp[4].k())
            V(C("tensor_copy", out=b3(an_b["a"], 8), in_=ang16.unsqueeze(2).to_broadcast([128, 16, 8])), r=allk, w=tmp[5].k())
            pbr, pbi = D_(tmp[6], n), D_(tmp[7], n)
            cpow(pbr, pbi, ar_b, an_b, ex, n, tmp[8], tmp[9], tmp[10], tmp[11])
            V(C("tensor_copy", out=L7[:, 0, :], in_=b3(pbr["a"], 8)[:, :, 7]), r=allk, w=["L7"])
            V(C("tensor_copy", out=L7[:, 1, :], in_=b3(pbi["a"], 8)[:, :, 7]), r=allk, w=["L7"])
            cre = s5cc.a[:, 0].rearrange("p (j c) -> p j c", c=32)
            cim = s5cc.a[:, 1].rearrange("p (j c) -> p j c", c=32)
            w1 = tmp[8].a[:, 0:512].rearrange("p (j c) -> p j c", c=32)
            w2 = tmp[9].a[:, 0:512].rearrange("p (j c) -> p j c", c=32)
            for b in range(8):
                lrb = b3(pbr["a"], 8)[:, :, b:b + 1].to_broadcast([128, 16, 32])
                lib = b3(pbi["a"], 8)[:, :, b:b + 1].to_broadcast([128, 16, 32])
                V(C("tensor_tensor", out=w1, in0=cre, in1=lrb, op=ALU.mult), r=allk, w=tmp[8].k())
                V(C("tensor_tensor", out=w2, in0=cim, in1=lib, op=ALU.mult), r=allk, w=tmp[9].k())
                V(C("tensor_tensor", out=Cm[:, :, b, 0, :], in0=w1, in1=w2, op=ALU.subtract), r=allk, w=["Cm"])
                V(C("tensor_tensor", out=w1, in0=cre, in1=lib, op=ALU.mult), r=allk, w=tmp[8].k())
                V(C("tensor_tensor", out=w2, in0=cim, in1=lrb, op=ALU.mult), r=allk, w=tmp[9].k())
                V(C("scalar_tensor_tensor", out=Cm[:, :, b, 1, :], in0=w1, scalar=-1.0, in1=w2, op0=ALU.mult, op1=ALU.subtract), r=allk, w=["Cm"])
            n = 1024
            ex = D_(tmp[3], n)
            ar_b, an_b = D_(tmp[4], n), D_(tmp[5], n)
            V(C("tensor_copy", out=b3(ex["a"], 64), in_=avals.unsqueeze(1).to_broadcast([128, 16, 64])), r=allk + ["cst"], w=tmp[3].k())
            V(C("tensor_copy", out=b3(ar_b["a"], 64), in_=ardt16.unsqueeze(2).to_broadcast([128, 16, 64])), r=allk, w=tmp[4].k())
            V(C("tensor_copy", out=b3(an_b["a"], 64), in_=ang16.unsqueeze(2).to_broadcast([128, 16, 64])), r=allk, w=tmp[5].k())
            pbr, pbi = D_(tmp[6], n), D_(tmp[7], n)
            cpow(pbr, pbi, ar_b, an_b, ex, n, tmp[8], tmp[9], tmp[10], tmp[11])
            V(C("tensor_copy", out=T2[:, 0], in_=b3(pbr["a"], 64)), r=allk, w=["T2"])
            V(C("tensor_copy", out=T2[:, 1], in_=b3(pbi["a"], 64)), r=allk, w=["T2"])
            V(C("tensor_scalar", out=ex["a"], in0=ex["a"], scalar1=-1.0, scalar2=8.0, op0=ALU.mult, op1=ALU.add), r=allk, w=tmp[3].k())
            cpow(pbr, pbi, ar_b, an_b, ex, n, tmp[8], tmp[9], tmp[10], tmp[11])
            V(C("tensor_copy", out=T1[:, 0], in_=b3(pbr["a"], 64)), r=allk, w=["T1"])
            V(C("tensor_copy", out=T1[:, 1], in_=b3(pbi["a"], 64)), r=allk, w=["T1"])

        s5_prep()

        wcnt = [0]
        kview = lambda wd, c0, c1, k0, kts: wd[k0 * 128:(k0 + kts) * 128, c0:c1].rearrange("(kt p) c -> p kt c", p=128)
        for g in range(9):
            P.dma("gpsimd", ws_d[g], kview(w_in_d, g * 512, (g + 1) * 512, 0, 8), writes=[("ws", g)])
        for nh in range(2):
            P.dma("gpsimd", ws_d[9 + nh], kview(w_out_d, nh * 512, (nh + 1) * 512, 0, 8), writes=[("ws", 9 + nh)])
        for g in range(11):
            P.dma("gpsimd", ws_d[11 + g][:, :, 0:256], kview(w_up_d, g * 256, (g + 1) * 256, 0, 8), writes=[("ws", 11 + g)])
            P.dma("gpsimd", ws_d[11 + g][:, :, 256:512], kview(w_up_d, DFF + g * 256, DFF + (g + 1) * 256, 0, 8), writes=[("ws", 11 + g)])
        for nh in range(2):
            for ci, (k0, kts) in enumerate(((0, 8), (8, 8), (16, 6))):
                gid = 22 + nh * 3 + ci
                P.dma("gpsimd", ws_d[gid][:, 0:kts, :], kview(w_down_d, nh * 512, (nh + 1) * 512, k0, kts), writes=[("ws", gid)])

        P.dma("gpsimd", ws_d[28][:, 0:4, :], w_glu_d.rearrange("(kt p) c -> p kt c", p=128), writes=[("ws", 28)])
        for gid_, wd_ in ((29, w_pa_d), (30, w_pb_d)):
            for half_ in range(2):
                P.dma("gpsimd", ws_d[gid_].rearrange("p (kt hf) c -> p kt hf c", hf=2)[:, :, half_, :],
                      wd_[:, half_ * 512:(half_ + 1) * 512].rearrange("(kt p) c -> p kt c", p=128), writes=[("ws", gid_)])

        def wfetch(gid, kts=8):
            i = wcnt[0] % NWB
            wcnt[0] += 1
            P.dma("sync", wbuf[i][:, 0:kts, :], ws_d[gid][:, 0:kts, :], reads=[("ws", gid)], writes=[("wb", i)])
            return i

        def norm_tile(tt, xsrc):
            A(C("activation", out=xs.a[:, tt], in_=xsrc.a[:, tt], func=AF.Square, accum_out=stat[:, tt:tt + 1]),
              r=xsrc.k(tt), w=xs.k(tt) + [("stat", tt)])
            A(C("activation", out=stat[:, 4 + tt:5 + tt], in_=stat[:, tt:tt + 1], func=AF.Ln, scale=1.0 / DM, bias=1e-6), r=[("stat", tt)], w=[("stat", tt)])
            A(C("activation", out=stat[:, 8 + tt:9 + tt], in_=stat[:, 4 + tt:5 + tt], func=AF.Exp, scale=-0.5), r=[("stat", tt)], w=[("stat", tt)])
            A(C("activation", out=xs.a[:, tt], in_=xsrc.a[:, tt], func=AF.Copy, scale=stat[:, 8 + tt:9 + tt]),
              r=xsrc.k(tt) + [("stat", tt)], w=xs.k(tt))

        def norm_to_uT(gcol, xsrc, tiles_done=False):
            if not tiles_done:
                for tt in range(4):
                    norm_tile(tt, xsrc)
            for dt in range(8):
                h = nbank()
                for tt in range(4):
                    T(C("transpose", out=pbh[h][:, tt * 128:(tt + 1) * 128],
                                                              in_=xs.a[:, tt, dt * 128:(dt + 1) * 128], identity=identb[:]),
                      r=xs.k(tt) + ["identb"], w=[pbk(h)], sig=(tt == 3))
                if dt % 2 == 0:
                    A(C("activation", out=uT.a[:, dt], in_=pbh[h][:, 0:512], func=AF.Copy, scale=pv[:, gcol + dt:gcol + dt + 1]),
                      r=[pbk(h), "pv"], w=uT.k(dt))
                else:
                    V(C("tensor_scalar", out=uT.a[:, dt], in0=pbh[h][:, 0:512],
                        scalar1=pv[:, gcol + dt:gcol + dt + 1], scalar2=None, op0=ALU.mult),
                      r=[pbk(h), "pv"], w=uT.k(dt))

        def mm_fm(bank, wb_i, mcol, rhs_buf, nk=8):
            for kt in range(nk):
                T(C("matmul", pb[bank][:], lhsT=wbuf[wb_i][:, kt, mcol * 128:(mcol + 1) * 128], rhs=rhs_buf.a[:, kt],
                                            start=(kt == 0), stop=(kt == nk - 1)),
                  r=[("wb", wb_i)] + rhs_buf.k(kt), w=[pbk(bank)], sig=(kt == nk - 1))

        dbg_slot = [0]

        def dbg_out(buf_ap, keys):
            if dbg and dbg_slot[0] < 8:
                P.dma("gpsimd", dbg_d[dbg_slot[0]], buf_ap, reads=keys, final=True)
                dbg_slot[0] += 1

        blocks = [(s_, b_) for s_ in range(nseq) for b_ in range(nblk)]
        xn = B_(72, [128, 4, DM], F32)
        obuf = B_(48, [128, 4, DM], F32)
        stat2 = sbt("stat2", [128, 16], F32)

        def prefetch_x(idx):
            s_, b_ = blocks[idx]
            for tt in range(4):
                P.dma("sync", xn.a[:, tt], x_d[s_, b_ * NT + tt * 128:b_ * NT + (tt + 1) * 128, :], writes=xn.k(tt))

        def xn_to_xt():
            for tt in range(4):
                P.op("gpsimd", C("tensor_copy", out=xt.a[:, tt], in_=xn.a[:, tt]), xn.k(tt), xt.k(tt))

        prefetch_x(0)
        norm_to_uT(GM, xn)
        xn_to_xt()
        for idx, (s, bi) in enumerate(blocks):
            if True:
                t0 = bi * NT
                first = bi == 0
                more = idx + 1 < len(blocks)
                if first:
                    V(C("memset", Sf[:], 0.0), w=[("Sf", h_) for h_ in range(4)])
                    V(C("memset", SbP[:], 0.0), w=[("SbP", h_) for h_ in range(4)])
                    V(C("memset", LS[:], 0.0), w=[(k_, i_) for i_ in range(4) for k_ in ("LSr", "LSi", "LSc")])
                    V(C("memset", muZ[:], 0.0), w=[("muZ", i_) for i_ in range(4)])
                    V(C("memset", hprev[:], 0.0), w=[("hprev", c_) for c_ in range(44)])

                wi = wfetch(0)
                for m in range(4):
                    b = nbank()
                    mm_fm(b, wi, m, uT)
                    A(C("activation", out=zaT.a[:, m], in_=pb[b][:], func=AF.Copy), r=[pbk(b)], w=zaT.k(m))
                wi = wfetch(1)
                for m in range(4):
                    b = nbank()
                    mm_fm(b, wi, m, uT)
                    A(C("activation", out=qa.a[:, m], in_=pb[b][:], func=AF.Silu), r=[pbk(b)], w=qa.k(m))
                wi = wfetch(2)
                for m in range(4):
                    b = nbank()
                    mm_fm(b, wi, m, uT)
                    A(C("activation", out=sf.a[:, m], in_=pb[b][:], func=AF.Sigmoid), r=[pbk(b)], w=sf.k(m))
                wi = wfetch(3)
                for tt in range(4):
                    b = nbank()
                    for kt in range(8):
                        T(C("matmul", pb[b][:], lhsT=uT.a[:, kt, tt * 128:(tt + 1) * 128], rhs=wbuf[wi][:, kt, :],
                                                                         start=(kt == 0), stop=(kt == 7)),
                          r=[("wb", wi)] + uT.k(kt), w=[pbk(b)], sig=(kt == 7))
                    V(C("tensor_copy", out=zi.a[:, tt], in_=pb[b][:]), r=[pbk(b)], w=zi.k(tt))
                wi = wfetch(4)
                for m in range(4):
                    b = nbank()
                    mm_fm(b, wi, m, uT)
                    A(C("activation", out=sg.a[:, m], in_=pb[b][:], func=AF.Silu), r=[pbk(b)], w=sg.k(m))

                hst = {}

                def h_A(h):
                    ebt, Kh, sT = ebts[h % 2], Khs[h % 2], sTs[h % 2]
                    lbc = lbv[:, h:h + 1]
                    omc = lbv[:, 4 + h:5 + h]
                    nomc = lbv[:, 8 + h:9 + h]
                    A(C("activation", out=logf.a, in_=sf.a[:, h], func=AF.Ln, scale=omc, bias=lbc),
                      r=sf.k(h) + ["lbv"], w=logf.k())
                    V(C("tensor_scalar", out=sf.a[:, h], in0=sf.a[:, h], scalar1=nomc, scalar2=omc, op0=ALU.mult, op1=ALU.add),
                      r=sf.k(h) + ["lbv"], w=sf.k(h))
                    V(C("tensor_tensor_scan", out=bcum.a, data0=mask64, data1=logf.a, initial=0.0, op0=ALU.mult, op1=ALU.add),
                      r=logf.k() + ["cst"], w=bcum.k())
                    A(C("activation", out=ebt.a, in_=bcum.a, func=AF.Exp), r=bcum.k(), w=ebt.k())
                    A(C("activation", out=enb.a, in_=bcum.a, func=AF.Exp, scale=-1.0), r=bcum.k(), w=enb.k())
                    V(C("scalar_tensor_tensor", out=qa.a[:, h], in0=qa.a[:, h], scalar=128.0 ** -0.5, in1=ebt.a, op0=ALU.mult, op1=ALU.mult),
                      r=qa.k(h) + ebt.k(), w=qa.k(h))
                    G(C("tensor_tensor", out=sf.a[:, h], in0=sf.a[:, h], in1=enb.a, op=ALU.mult), r=sf.k(h) + enb.k(), w=sf.k(h))
                    eb3 = ebt.a.rearrange("p (c s) -> p c s", s=64)
                    G(C("tensor_tensor", out=KhT.a.rearrange("p (c s) -> p c s", s=64), in0=sf.a[:, h].rearrange("p (c s) -> p c s", s=64),
                        in1=eb3[:, :, 63:64].to_broadcast([128, 8, 64]), op=ALU.mult),
                      r=sf.k(h) + ebt.k(), w=KhT.k())
                    bt = nbank(hold=True)
                    for tt in range(4):
                        T(C("transpose", out=pbh[bt][:, tt * 128:(tt + 1) * 128], in_=KhT.a[:, tt * 128:(tt + 1) * 128], identity=identb[:]),
                          r=KhT.k() + ["identb"], w=[pbk(bt)], sig=(tt == 3))
                    A(C("activation", out=Kh.a, in_=pbh[bt][:, 0:512].rearrange("p (a b) -> p a b", b=128), func=AF.Copy), r=[pbk(bt)], w=Kh.k())
                    held.discard(bt)
                    bs = nbank(hold=True)
                    for tt in range(4):
                        T(C("matmul", pb[bs][:, tt * 128:(tt + 1) * 128], lhsT=sf.a[:, h, tt * 128:(tt + 1) * 128],
                            rhs=qa.a[:, h, tt * 128:(tt + 1) * 128], start=True, stop=True),
                          r=sf.k(h) + qa.k(h), w=[pbk(bs)], sig=(tt == 3))
                    V(C("tensor_tensor", out=sT.a, in0=pb[bs][:], in1=maskblk, op=ALU.mult), r=[pbk(bs), "cst"], w=sT.k())
                    held.discard(bs)

                def h_B(h):
                    ebt, Kh, sT = ebts[h % 2], Khs[h % 2], sTs[h % 2]
                    bo = nbank(hold=True)
                    hst["bo"] = bo
                    kvb = {}

                    def kv(c):
                        tt, hf = c // 2, c % 2
                        bk_ = nbank(hold=True)
                        kvb[c] = bk_
                        T(C("matmul", pb[bk_][:, 0:128], lhsT=Kh.a[hf * 64:(hf + 1) * 64, tt, :],
                            rhs=zi.a[hf * 64:(hf + 1) * 64, tt, h * 128:(h + 1) * 128], start=True, stop=True),
                          r=Kh.k() + zi.k(tt), w=[pbk(bk_)])

                    kv(0)
                    for c in range(8):
                        tt, hf = c // 2, c % 2
                        sprev = SbP[:, h, :] if c == 0 else Sb8[:, c, :]
                        skey = ("SbP", h) if c == 0 else ("Sb8", c)
                        T(C("matmul", pb[bo][:, c * 64:(c + 1) * 64], lhsT=sprev, rhs=qa.a[:, h, c * 64:(c + 1) * 64],
                            start=(c == 0), stop=False),
                          r=[skey] + qa.k(h), w=[pbk(bo)], sig=False)
                        if hf == 1:
                            T(C("matmul", pb[bo][:, tt * 128:(tt + 1) * 128], lhsT=zi.a[:, tt, h * 128:(h + 1) * 128],
                                rhs=sT.a[:, tt * 128:(tt + 1) * 128], start=False, stop=(tt == 3)),
                              r=zi.k(tt) + sT.k(), w=[pbk(bo)], sig=(tt == 3))
                        if c + 1 < 8:
                            kv(c + 1)
                        bk_ = kvb[c]
                        ebl = ebt.a[:, c * 64 + 63:c * 64 + 64]
                        V(C("scalar_tensor_tensor", out=Sf[:, h, :], in0=Sf[:, h, :], scalar=ebl, in1=pb[bk_][:, 0:128],
                            op0=ALU.mult, op1=ALU.add),
                          r=[("Sf", h), pbk(bk_)] + ebt.k(), w=[("Sf", h)])
                        held.discard(bk_)
                        if c < 7:
                            A(C("activation", out=Sb8[:, c + 1, :], in_=Sf[:, h, :], func=AF.Copy), r=[("Sf", h)], w=[("Sb8", c + 1)])
                        else:
                            A(C("activation", out=SbP[:, h, :], in_=Sf[:, h, :], func=AF.Copy), r=[("Sf", h)], w=[("SbP", h)])

                def h_C(h):
                    bo = hst["bo"]
                    A(C("activation", out=osq.a, in_=pb[bo][:], func=AF.Square), r=[pbk(bo)], w=osq.k())
                    bm_ = nbank(hold=True)
                    T(C("matmul", pb[bm_][:], lhsT=onesb[:], rhs=osq.a, start=True, stop=True), r=["onesb"] + osq.k(), w=[pbk(bm_)])
                    A(C("activation", out=hrs.a, in_=pb[bm_][:], func=AF.Ln, scale=1.0 / 128, bias=1e-6), r=[pbk(bm_)], w=hrs.k())
                    held.discard(bm_)
                    A(C("activation", out=hrs.a, in_=hrs.a, func=AF.Exp, scale=-0.5), r=hrs.k(), w=hrs.k())
                    V(C("scalar_tensor_tensor", out=ytm.a, in0=pb[bo][:], scalar=pv[:, HG + h:HG + h + 1], in1=hrs.a, op0=ALU.mult, op1=ALU.mult),
                      r=[pbk(bo), "pv"] + hrs.k(), w=ytm.k())
                    G(C("tensor_tensor", out=ybT.a[:, h], in0=ytm.a, in1=sg.a[:, h], op=ALU.mult), r=ytm.k() + sg.k(h), w=ybT.k(h))
                    held.discard(bo)

                def s5_A(i):
                    Vs = Vsb[i % 2]
                    jz = i
                    za3 = zaT.a[:, jz].rearrange("p (a b) -> p a b", b=8)
                    for pl in range(2):
                        for j0 in (0, 2):
                            bks = {jl: nbank(hold=True) for jl in (j0, j0 + 1)}
                            atomic(True)
                            for b in range(8):
                                for jl in (j0, j0 + 1):
                                    q = jl
                                    T(C("matmul", pb[bks[jl]][:, b * 64:512], lhsT=Bm[32 * q:32 * q + 32, jz, b, pl, :],
                                        rhs=za3[32 * q:32 * q + 32, :, b].unsqueeze(1).to_broadcast([32, 8 - b, 64]),
                                        start=(b == 0), stop=(b == 7), tile_position=(32 * q, 0)),
                                      r=["Bm"] + zaT.k(jz), w=[pbk(bks[jl])], sig=(b == 7))
                            atomic(False)
                            for jl in (j0, j0 + 1):
                                A(C("activation", out=Vs.a[:, jl, pl], in_=pb[bks[jl]][:], func=AF.Copy), r=[pbk(bks[jl])], w=Vs.k(jl))
                                held.discard(bks[jl])

                def s5_B(i):
                    Vs = Vsb[i % 2]
                    js = slice(4 * i, 4 * i + 4)
                    Ere = Vs.a[:, :, 0, 448:512]
                    Eim = Vs.a[:, :, 1, 448:512]
                    t1r, t1i = T1[:, 0, js, :], T1[:, 1, js, :]
                    t2r, t2i = T2[:, 0, js, :], T2[:, 1, js, :]
                    Bmre, Bmim, Bu1, Bu2, Bpre, Bpim = l2
                    Bu3, Bu4 = l2x
                    mre, mim, u1, u2, pre, pim = [b.a for b in l2]
                    u3, u4 = Bu3.a, Bu4.a
                    vk = Vs.k()
                    KR, KI, KC, KM = [("LSr", i)], [("LSi", i)], [("LSc", i)], [("muZ", i)]
                    G(C("tensor_tensor", out=u1, in0=t1r, in1=Ere, op=ALU.mult), r=vk + ["T1"], w=Bu1.k())
                    V(C("tensor_tensor", out=u3, in0=t1r, in1=Eim, op=ALU.mult), r=vk + ["T1"], w=Bu3.k())
                    G(C("tensor_tensor", out=u2, in0=t1i, in1=Eim, op=ALU.mult), r=vk + ["T1"], w=Bu2.k())
                    V(C("tensor_tensor", out=u4, in0=t1i, in1=Ere, op=ALU.mult), r=vk + ["T1"], w=Bu4.k())
                    G(C("tensor_tensor", out=mre, in0=u1, in1=u2, op=ALU.subtract), r=Bu1.k() + Bu2.k(), w=Bmre.k())
                    V(C("tensor_tensor", out=mim, in0=u3, in1=u4, op=ALU.add), r=Bu3.k() + Bu4.k(), w=Bmim.k())
                    fl = lambda ap: ap.rearrange("p a b -> p (a b)")
                    V(C("tensor_tensor_scan", out=fl(pim), data0=mask64[:, 0:256], data1=fl(mim), initial=0.0, op0=ALU.mult, op1=ALU.add), r=Bmim.k() + ["cst"], w=Bpim.k())
                    V(C("tensor_tensor_scan", out=fl(pre), data0=mask64[:, 0:256], data1=fl(mre), initial=0.0, op0=ALU.mult, op1=ALU.add), r=Bmre.k() + ["cst"], w=Bpre.k())
                    G(C("tensor_tensor", out=pre, in0=pre, in1=muZ[:, js, 0:1].to_broadcast([128, 4, 64]), op=ALU.add), r=Bpre.k() + KM, w=Bpre.k())
                    V(C("tensor_tensor", out=pim, in0=pim, in1=muZ[:, js, 1:2].to_broadcast([128, 4, 64]), op=ALU.add), r=Bpim.k() + KM, w=Bpim.k())
                    lsr = LS[:, js, 0, 1:65]
                    lsi = LS[:, js, 1, 1:65]
                    G(C("tensor_tensor", out=u1, in0=t2r, in1=pre, op=ALU.mult), r=Bpre.k() + ["T2"], w=Bu1.k())
                    V(C("tensor_tensor", out=u3, in0=t2r, in1=pim, op=ALU.mult), r=Bpim.k() + ["T2"], w=Bu3.k())
                    G(C("tensor_tensor", out=u2, in0=t2i, in1=pim, op=ALU.mult), r=Bpim.k() + ["T2"], w=Bu2.k())
                    V(C("tensor_tensor", out=u4, in0=t2i, in1=pre, op=ALU.mult), r=Bpre.k() + ["T2"], w=Bu4.k())
                    G(C("tensor_tensor", out=lsr, in0=u1, in1=u2, op=ALU.subtract), r=Bu1.k() + Bu2.k(), w=KR)
                    V(C("tensor_tensor", out=lsi, in0=u3, in1=u4, op=ALU.add), r=Bu3.k() + Bu4.k(), w=KI)
                    V(C("tensor_copy", out=LSb.a, in_=LS[:, js, :, 0:64]), r=KR + KI + KC, w=LSb.k())
                    G(C("tensor_copy", out=LS[:, js, :, 0:1], in_=LS[:, js, :, 64:65]), r=KR + KI + LSb.k(), w=KC)
                    zr = LS[:, js, 0, 0]
                    zim = LS[:, js, 1, 0]
                    l7r, l7i = L7[:, 0, js], L7[:, 1, js]
                    s_a, s_b, s_c, s_d = mzt[:, 0, js], mzt[:, 1, js], mzt[:, 2, js], mzt[:, 3, js]
                    KT = [("mzt", i)]
                    G(C("tensor_tensor", out=s_a, in0=l7r, in1=zr, op=ALU.mult), r=KC + ["L7"], w=KT)
                    G(C("tensor_tensor", out=s_b, in0=l7i, in1=zim, op=ALU.mult), r=KC + ["L7"], w=KT)
                    G(C("tensor_tensor", out=muZ[:, js, 0], in0=s_a, in1=s_b, op=ALU.subtract), r=KT, w=KM)
                    G(C("tensor_tensor", out=s_c, in0=l7r, in1=zim, op=ALU.mult), r=KC + ["L7"], w=KT)
                    G(C("tensor_tensor", out=s_d, in0=l7i, in1=zr, op=ALU.mult), r=KC + ["L7"], w=KT)
                    G(C("tensor_tensor", out=muZ[:, js, 1], in0=s_c, in1=s_d, op=ALU.add), r=KT, w=KM)

                def s5_C(i):
                    Vs = Vsb[i % 2]
                    jz = i
                    by = nbank(hold=True)
                    cnt = 0
                    atomic(True)
                    for b in range(8):
                        for pl in range(2):
                            for src in range(2):
                                for q in range(4):
                                    jl = q
                                    j = 4 * i + jl
                                    rhs = Vs.a[:, jl, pl, b * 64:(b + 1) * 64] if src == 0 else LSb.a[:, jl, pl, :]
                                    T(C("matmul", pb[by][32 * q:32 * q + 32, b::8], lhsT=Cm[:, j, b, pl, :], rhs=rhs,
                                        start=(cnt == 0), stop=(cnt == 31), tile_position=(0, 32 * q)),
                                      r=["Cm"] + Vs.k(jl) + LSb.k(), w=[pbk(by)], sig=(cnt == 31 and q == 3))
                                cnt += 1
                    atomic(False)
                    V(C("scalar_tensor_tensor", out=yv.a, in0=zaT.a[:, jz], scalar=pv[:, SD + jz:SD + jz + 1], in1=pb[by][:], op0=ALU.mult, op1=ALU.add),
                      r=zaT.k(jz) + [pbk(by), "pv"], w=yv.k())
                    held.discard(by)
                    A(C("activation", out=yaP.a[:, jz], in_=yv.a, func=AF.Gelu_apprx_tanh), r=yv.k(), w=yaP.k(jz))

                Interleave([lambda: s5_A(0), lambda: h_A(0)]).run()
                for i in range(4):
                    chains = [lambda i=i: (s5_B(i), s5_C(i)), lambda i=i: (h_B(i), h_C(i))]
                    if i < 3:
                        chains += [lambda i=i: s5_A(i + 1), lambda i=i: h_A(i + 1)]
                    Interleave(chains).run()
                wgl = wfetch(28, kts=4)
                for ct in range(4):
                    b = nbank()
                    for kt in range(4):
                        T(C("matmul", pb[b][:], lhsT=wbuf[wgl][:, kt, ct * 128:(ct + 1) * 128], rhs=yaP.a[:, kt], start=(kt == 0), stop=(kt == 3)),
                          r=[("wb", wgl)] + yaP.k(kt), w=[pbk(b)], sig=(kt == 3))
                    A(C("activation", out=gsig.a, in_=pb[b][:], func=AF.Sigmoid, bias=pv[:, BG + ct:BG + ct + 1]), r=[pbk(b), "pv"], w=gsig.k())
                    V(C("tensor_tensor", out=yaT.a[:, ct], in0=yaP.a[:, ct], in1=gsig.a, op=ALU.mult), r=yaP.k(ct) + gsig.k(), w=yaT.k(ct))

                if dbg == 1:
                    for i_ in range(4):
                        dbg_out(yaT.a[:, i_], yaT.k(i_))
                    for i_ in range(4):
                        dbg_out(ybT.a[:, i_], ybT.k(i_))
                if dbg == 2:
                    for i_ in range(4):
                        dbg_out(yaP.a[:, i_], yaP.k(i_))
                    for i_ in range(4):
                        dbg_out(zaT.a[:, i_], zaT.k(i_))
                for pas, (gw, gg, ysrc, sbuf_) in enumerate(((29, 5, yaT, siga), (30, 7, ybT, sigb))):
                    wg0_ = wfetch(gg)
                    wp_ = wfetch(gw)
                    for half in range(2):
                        wg_ = wg0_ if half == 0 else wfetch(gg + half)
                        for m in range(4):
                            ct = half * 4 + m
                            b1 = nbank()
                            mm_fm(b1, wg_, m, uT)
                            A(C("activation", out=sbuf_.a, in_=pb[b1][:], func=AF.Sigmoid), r=[pbk(b1)], w=sbuf_.k())
                            b2 = nbank()
                            for kt in range(4):
                                T(C("matmul", pb[b2][:], lhsT=wbuf[wp_][:, 2 * kt + half, m * 128:(m + 1) * 128], rhs=ysrc.a[:, kt], start=(kt == 0), stop=(kt == 3)),
                                  r=[("wb", wp_)] + ysrc.k(kt), w=[pbk(b2)], sig=(kt == 3))
                            if pas == 0:
                                V(C("tensor_tensor", out=mT.a[:, ct], in0=pb[b2][:], in1=sbuf_.a, op=ALU.mult), r=[pbk(b2)] + sbuf_.k(), w=mT.k(ct))
                            else:
                                V(C("tensor_tensor", out=t1.a, in0=pb[b2][:], in1=sbuf_.a, op=ALU.mult), r=[pbk(b2)] + sbuf_.k(), w=t1.k())
                                G(C("tensor_tensor", out=mT.a[:, ct], in0=t1.a, in1=mT.a[:, ct], op=ALU.add), r=t1.k() + mT.k(ct), w=mT.k(ct))

                wos = [wfetch(9), wfetch(10)]
                for tt in range(4):
                    for nh in range(2):
                        wo = wos[nh]
                        b = nbank()
                        for kt in range(8):
                            T(C("matmul", pb[b][:], lhsT=mT.a[:, kt, tt * 128:(tt + 1) * 128], rhs=wbuf[wo][:, kt, :], start=(kt == 0), stop=(kt == 7)),
                              r=[("wb", wo)] + mT.k(kt), w=[pbk(b)], sig=(kt == 7))
                        V(C("tensor_tensor", out=xt.a[:, tt, nh * 512:(nh + 1) * 512], in0=pb[b][:], in1=xt.a[:, tt, nh * 512:(nh + 1) * 512], op=ALU.add),
                          r=[pbk(b)] + xt.k(tt), w=xt.k(tt))
                    norm_tile(tt, xt)

                norm_to_uT(GF, xt, tiles_done=True)
                if more:
                    prefetch_x(idx + 1)
                for g in range(11):
                    wu = wfetch(11 + g)
                    for pi in range(2):
                        m = 2 * g + pi
                        par = m % 2
                        for gv in range(2):
                            col = m + 22 * gv
                            b = nbank()
                            mm_fm(b, wu, gv * 2 + pi, uT)
                            hb = hs[par][gv]
                            c0 = cc[par][gv]
                            P.op("gpsimd", C("tensor_copy", out=hb.a[:, 0:2], in_=hprev[:, col, :]), [("hprev", col)], hb.k())
                            A(C("activation", out=hb.a[:, 2:514], in_=pb[b][:], func=AF.Copy), r=[pbk(b)], w=hb.k())
                            A(C("activation", out=c0.a, in_=pb[b][:], func=AF.Identity, scale=pv[:, WC2 + col:WC2 + col + 1], bias=pv[:, BC + col:BC + col + 1]),
                              r=[pbk(b), "pv"], w=c0.k())
                            V(C("scalar_tensor_tensor", out=c0.a, in0=hb.a[:, 1:513], scalar=pv[:, WC1 + col:WC1 + col + 1], in1=c0.a, op0=ALU.mult, op1=ALU.add),
                              r=hb.k() + c0.k() + ["pv"], w=c0.k())
                            V(C("scalar_tensor_tensor", out=c0.a, in0=hb.a[:, 0:512], scalar=pv[:, WC0 + col:WC0 + col + 1], in1=c0.a, op0=ALU.mult, op1=ALU.add),
                              r=hb.k() + c0.k() + ["pv"], w=c0.k())
                            P.op("gpsimd", C("tensor_copy", out=hprev[:, col, :], in_=hb.a[:, 512:514]), hb.k(), [("hprev", col)])
                        A(C("activation", out=gact[par].a, in_=cc[par][0].a, func=AF.Silu), r=cc[par][0].k(), w=gact[par].k())
                        G(C("tensor_tensor", out=aT.a[:, m], in0=gact[par].a, in1=cc[par][1].a, op=ALU.mult), r=gact[par].k() + cc[par][1].k(), w=aT.k(m))
                if more:
                    for tt_ in range(4):
                        norm_tile(tt_, xn)
                for nh in range(2):
                    banks = [nbank(hold=True) for _ in range(4)]
                    for ci, (k0, kts) in enumerate(((0, 8), (8, 8), (16, 6))):
                        wd = wfetch(22 + nh * 3 + ci, kts=kts)
                        for tt in range(4):
                            for kt in range(kts):
                                kk = k0 + kt
                                T(C("matmul", pb[banks[tt]][:], lhsT=aT.a[:, kk, tt * 128:(tt + 1) * 128], rhs=wbuf[wd][:, kt, :],
                                                                                   start=(kk == 0), stop=(kk == 21)),
                                  r=[("wb", wd)] + aT.k(kk), w=[pbk(banks[tt])], sig=(kt == kts - 1))
                    for tt in range(4):
                        b = banks[tt]
                        V(C("tensor_tensor", out=xt.a[:, tt, nh * 512:(nh + 1) * 512], in0=pb[b][:], in1=xt.a[:, tt, nh * 512:(nh + 1) * 512], op=ALU.add),
                          r=[pbk(b)] + xt.k(tt), w=xt.k(tt))
                        held.discard(b)
                    if nh == 0 and more:
                        norm_to_uT(GM, xn, tiles_done=True)
                for tt in range(4):
                    A(C("activation", out=obuf.a[:, tt], in_=xt.a[:, tt], func=AF.Square, accum_out=stat2[:, tt:tt + 1]), r=xt.k(tt), w=obuf.k(tt) + ["stat2"])
                A(C("activation", out=stat2[:, 4:8], in_=stat2[:, 0:4], func=AF.Ln, scale=1.0 / DM, bias=1e-6), r=["stat2"], w=["stat2"])
                A(C("activation", out=stat2[:, 8:12], in_=stat2[:, 4:8], func=AF.Exp, scale=-0.5), r=["stat2"], w=["stat2"])
                for tt in range(4):
                    V(C("scalar_tensor_tensor", out=obuf.a[:, tt], in0=xt.a[:, tt], scalar=stat2[:, 8 + tt:9 + tt], in1=gfin[:], op0=ALU.mult, op1=ALU.mult),
                      r=xt.k(tt) + ["stat2", "gfin"], w=obuf.k(tt))
                    P.dma("gpsimd", out_d[s, t0 + tt * 128:t0 + (tt + 1) * 128, :], obuf.a[:, tt], reads=obuf.k(tt), final=True)
                if more:
                    xn_to_xt()
        P.finalize()
    return nc


def _pvec(v):
    v = np.asarray(v, np.float32).reshape(-1)
    return np.ascontiguousarray(v.reshape(-1, 128).T)


def host_layout(inp):
    f = lambda k: np.asarray(inp[k], np.float32)
    pvcols = [_pvec(f("g_mix")[0]), _pvec(f("g_ffn")[0]), _pvec(f("b_glu")[0]), _pvec(f("s5_d")[0]), _pvec(f("hg_norm_gain")[0]),
              _pvec(f("hg_lb_logits")[0]), _pvec(f("hg_lb_logits")[1]), _pvec(f("w_conv")[0, 0]), _pvec(f("w_conv")[0, 1]),
              _pvec(f("w_conv")[0, 2]), _pvec(f("b_conv")[0])]
    pv = np.ascontiguousarray(np.concatenate(pvcols, axis=1))
    assert pv.shape == (128, NPV)
    gfin = np.ascontiguousarray(np.broadcast_to(f("g_final")[None, :], (128, DM)))
    a_re, a_im, ldt = f("s5_a_re")[0], f("s5_a_im")[0], f("s5_log_dt")[0]
    b_re, b_im = f("s5_b_re")[0], f("s5_b_im")[0]
    c_re, c_im = f("s5_c_re")[0], f("s5_c_im")[0]
    s5s = np.zeros((128, 3, 16), np.float32)
    for j in range(16):
        for g2 in range(2):
            g = 2 * j + g2
            s5s[g2 * 64:(g2 + 1) * 64, 0, j] = a_re[g]
            s5s[g2 * 64:(g2 + 1) * 64, 1, j] = a_im[g]
            s5s[g2 * 64:(g2 + 1) * 64, 2, j] = ldt[g]
    s5b = np.zeros((128, 3, 4, 2, 64), np.float32)
    s5bb = np.zeros((128, 2, 4, 2, 64), np.float32)
    for q in range(4):
        for jz in range(4):
            for g2 in range(2):
                g = 2 * (4 * jz + q) + g2
                s5b[32 * q:32 * q + 32, 0, jz, g2, :] = a_re[g][None, :]
                s5b[32 * q:32 * q + 32, 1, jz, g2, :] = a_im[g][None, :]
                s5b[32 * q:32 * q + 32, 2, jz, g2, :] = ldt[g]
                r0 = 32 * q + 16 * g2
                s5bb[r0:r0 + 16, 0, jz, g2, :] = b_re[g].T
                s5bb[r0:r0 + 16, 1, jz, g2, :] = b_im[g].T
    s5cc = np.zeros((128, 2, 16, 2, 16), np.float32)
    for j in range(16):
        for g2 in range(2):
            g = 2 * j + g2
            s5cc[g2 * 64:(g2 + 1) * 64, 0, j, g2, :] = c_re[g].T
            s5cc[g2 * 64:(g2 + 1) * 64, 1, j, g2, :] = c_im[g].T
    cst = np.zeros((128, 128 + 3 * 512 + 64 + 8), np.float32)
    cst[:, 0:128] = np.eye(128, dtype=np.float32)
    sidx = np.arange(128)[:, None]
    tidx = np.arange(128)[None, :]
    blk = ((sidx // 64) == (tidx // 64)) & (sidx <= tidx)
    cst[:, 128:640] = np.tile(blk.astype(np.float32), (1, 4))
    m64 = np.ones(512, np.float32)
    m64[::64] = 0
    m8 = np.ones(512, np.float32)
    m8[::8] = 0
    cst[:, 640:1152] = m64[None, :]
    cst[:, 1152:1664] = m8[None, :]
    cst[:, 1664:1728] = (8.0 * np.arange(64) + 1.0)[None, :]
    cst[:, 1728:1736] = np.arange(8, dtype=np.float32)[None, :]
    shared = dict(
        w_in=np.ascontiguousarray(f("w_in")[0]), w_glu=np.ascontiguousarray(f("w_glu")[0]), w_pa=np.ascontiguousarray(f("w_pa")[0]),
        w_pb=np.ascontiguousarray(f("w_pb")[0]), w_out=np.ascontiguousarray(f("w_out")[0]), w_up=np.ascontiguousarray(f("w_up")[0]),
        w_down=np.ascontiguousarray(f("w_down")[0]), pv=pv, gfin=gfin, s5s=s5s,
        s5b=np.ascontiguousarray(s5b.reshape(128, 3, 512)), s5bb=np.ascontiguousarray(s5bb.reshape(128, 2, 512)),
        s5cc=np.ascontiguousarray(s5cc.reshape(128, 2, 512)), cst=cst)
    return shared


_NC_CACHE = {}


def kernel(**inputs):
    x = np.asarray(inputs["x"], np.float32)
    shared = host_layout(inputs)
    if "nc" not in _NC_CACHE:
        _NC_CACHE["nc"] = build_program()
    nc = _NC_CACHE["nc"]
    in_maps = []
    for c in range(NCORES):
        m = dict(shared)
        m["x"] = np.ascontiguousarray(x[2 * c:2 * c + 2])
        in_maps.append(m)
    res = run_bass_kernel_spmd(nc, in_maps, core_ids=list(range(NCORES)))
    out = np.concatenate([np.asarray(r["out"], np.float32) for r in res.results], axis=0)
    return out
```

```python
import math
import threading
from contextlib import ExitStack

import numpy as np
import concourse.bass as bass
import concourse.mybir as mybir
from concourse.bass_utils import run_bass_kernel_spmd

F32 = mybir.dt.float32
BF16 = mybir.dt.bfloat16
I32 = mybir.dt.int32
AF = mybir.ActivationFunctionType
ALU = mybir.AluOpType

ENGS = ["tensor", "vector", "scalar", "gpsimd", "sync"]
RING = 8
NCORES = 8
SEQ = 2048
DM = 1024
NT = 512
NBLK = SEQ // NT
DFF = 2816
TWO_PI = 2.0 * math.pi
TWO_PI_HI = 6.2831854820251465
TWO_PI_LO = -1.7484556025237907e-07

GM, GF, BG, SD, HG, L0, L1, WC0, WC1, WC2, BC, NPV = 0, 8, 16, 20, 24, 28, 32, 36, 80, 124, 168, 212


class Prog:
    def __init__(self, nc, ctx):
        self.nc = nc
        self.q = {e: [] for e in ENGS}
        self.sem = {e: ctx.enter_context(nc.semaphore("s_" + e)) for e in ENGS}
        self.ring = {
            e: [ctx.enter_context(nc.semaphore("d_%s_%d" % (e, i))) for i in range(RING)]
            for e in ("sync", "gpsimd")
        }
        self.ring_cnt = {e: [0] * RING for e in ("sync", "gpsimd")}
        self.ring_pos = {e: 0 for e in ("sync", "gpsimd")}
        self.lastw = {}
        self.readers = {}
        self.final_events = []

    def _deps(self, eng, reads, writes):
        deps = set()
        for r in reads:
            ev = self.lastw.get(r)
            if ev is not None:
                deps.add(ev)
        for w in writes:
            ev = self.lastw.get(w)
            if ev is not None:
                deps.add(ev)
            for ev in self.readers.get(w, ()):
                deps.add(ev)
        return deps

    def _commit(self, ev, reads, writes):
        for r in reads:
            self.readers.setdefault(r, []).append(ev)
        for w in writes:
            self.lastw[w] = ev
            self.readers[w] = []

    def op(self, eng, fn, reads=(), writes=(), sig=True):
        deps = self._deps(eng, reads, writes)
        ev = ("e", eng, len(self.q[eng]))
        self.q[eng].append(dict(deps=deps, fn=fn, sig=sig, kind="op"))
        self._commit(ev, reads, writes)
        if Interleave.current is not None:
            Interleave.current.switch()
        return ev

    def dma(self, eng, out, in_, reads=(), writes=(), final=False):
        deps = self._deps(eng, reads, writes)
        s = self.ring_pos[eng]
        self.ring_pos[eng] = (s + 1) % RING
        sem = self.ring[eng][s]
        prev = self.ring_cnt[eng][s]
        if prev > 0:
            deps.add(("d", sem, prev))
        self.ring_cnt[eng][s] = prev + 16
        ev = ("d", sem, prev + 16)
        self.q[eng].append(dict(deps=deps, out=out, in_=in_, sem=sem, kind="dma"))
        self._commit(ev, reads, writes)
        if final:
            self.final_events.append(ev)
        return ev

    def finalize(self):
        nc = self.nc
        self.q["sync"].append(dict(deps=set(self.final_events), kind="nop"))
        sigcount = {}
        for e in ENGS:
            for ins in reversed(self.q[e]):
                if ins["kind"] == "op":
                    ins["sig"] = True
                    break
            c = 0
            arr = []
            for ins in self.q[e]:
                if ins["kind"] == "op" and ins["sig"]:
                    c += 1
                arr.append(c)
            sigcount[e] = arr

        def resolve(ev):
            if ev[0] == "d":
                return (ev[1], ev[2])
            _, eng, idx = ev
            q = self.q[eng]
            j = idx
            while not (q[j]["kind"] == "op" and q[j]["sig"]):
                j += 1
            return (self.sem[eng], sigcount[eng][j])

        with nc.Block() as block:
            for e in ENGS:
                def body(engine, e=e):
                    seen = {}
                    for ins in self.q[e]:
                        for ev in ins["deps"]:
                            if ev[0] == "e" and ev[1] == e and e == "tensor":
                                continue
                            sem, val = resolve(ev)
                            k = id(sem)
                            if seen.get(k, 0) >= val:
                                continue
                            seen[k] = val
                            engine.wait_ge(sem, val)
                        if ins["kind"] == "op":
                            r = ins["fn"](engine)
                            if ins["sig"]:
                                r.then_inc(self.sem[e], 1)
                        elif ins["kind"] == "dma":
                            engine.dma_start(out=ins["out"], in_=ins["in_"]).then_inc(ins["sem"], 16)

                getattr(block, e)(body)


class Interleave:
    current = None

    def __init__(self, fns):
        self.fns = fns
        self.n = len(fns)
        self.turn = 0
        self.alive = [True] * self.n
        self.cv = threading.Condition()
        self.err = None
        self.tl = threading.local()

    def _next(self, i):
        for d in range(1, self.n + 1):
            j = (i + d) % self.n
            if self.alive[j]:
                return j
        return -1

    def _wrap(self, i):
        with self.cv:
            while self.turn != i:
                self.cv.wait()
        self.tl.me = i
        try:
            self.fns[i]()
        except BaseException as ex:
            self.err = ex
        finally:
            with self.cv:
                self.alive[i] = False
                self.turn = self._next(i)
                self.cv.notify_all()

    def switch(self):
        i = getattr(self.tl, "me", None)
        if i is None or getattr(self.tl, "atomic", 0) > 0:
            return
        with self.cv:
            nxt = self._next(i)
            if nxt == i or nxt < 0:
                return
            self.turn = nxt
            self.cv.notify_all()
            while self.turn != i:
                self.cv.wait()

    def atomic(self, on):
        self.tl.atomic = getattr(self.tl, "atomic", 0) + (1 if on else -1)

    def run(self):
        Interleave.current = self
        ths = [threading.Thread(target=self._wrap, args=(i,)) for i in range(self.n)]
        for t in ths:
            t.start()
        for t in ths:
            t.join()
        Interleave.current = None
        if self.err is not None:
            raise self.err


def C(name, *a, **k):
    return lambda e: getattr(e, name)(*a, **k)


class Buf:
    def __init__(self, arena, off_kb, shape, dt):
        self.shape = shape
        self.dt = dt
        esz = 4 if dt in (F32, I32) else 2
        n = int(np.prod(shape[1:]))
        self.nbytes = n * esz
        self.off = off_kb * 1024
        a = arena[:, self.off // 2:(self.off + self.nbytes) // 2]
        if esz == 4:
            a = a.bitcast(dt)
        if len(shape) == 3:
            a = a.rearrange("p (a b) -> p a b", b=shape[2])
        elif len(shape) == 4:
            a = a.rearrange("p (a b c) -> p a b c", b=shape[2], c=shape[3])
        self.a = a
        self.sub = (self.nbytes // shape[1]) if len(shape) >= 3 else self.nbytes

    def k(self, i=None, n=1):
        if i is None:
            lo, hi = self.off, self.off + self.nbytes
        else:
            lo = self.off + i * self.sub
            hi = lo + n * self.sub
        return [("ar", g) for g in range(lo // 1024, (hi + 1023) // 1024)]


def build_program(nblk=NBLK, nseq=2, dbg=False):
    nc = bass.Bass("TRN2", target_bir_lowering=False)
    dram = lambda name, shape, kind="ExternalInput": nc.dram_tensor(name, shape, F32, kind=kind).ap()
    x_d = dram("x", [2, SEQ, DM])
    w_in_d = dram("w_in", [DM, 4608])
    w_glu_d = dram("w_glu", [512, 512])
    w_pa_d = dram("w_pa", [512, DM])
    w_pb_d = dram("w_pb", [512, DM])
    w_out_d = dram("w_out", [DM, DM])
    w_up_d = dram("w_up", [DM, 2 * DFF])
    w_down_d = dram("w_down", [DFF, DM])
    pv_d = dram("pv", [128, NPV])
    gfin_d = dram("gfin", [128, DM])
    s5s_d = dram("s5s", [128, 3, 16])
    s5b_d = dram("s5b", [128, 3, 512])
    s5bb_d = dram("s5bb", [128, 2, 512])
    s5cc_d = dram("s5cc", [128, 2, 512])
    cst_d = dram("cst", [128, 128 + 3 * 512 + 64 + 8])
    out_d = dram("out", [2, SEQ, DM], kind="ExternalOutput")
    NG = 31
    ws_d = nc.dram_tensor("wscratch", [NG, 128, 8, 512], BF16, kind="Internal").ap()
    if dbg:
        dbg_d = dram("dbg", [8, 128, 512], kind="ExternalOutput")

    ctx = ExitStack()
    with ctx:
        P = Prog(nc, ctx)
        sbt = lambda name, shape, dt: ctx.enter_context(nc.sbuf_tensor("sb_" + name, shape, dt))
        pst = lambda name, shape, dt: ctx.enter_context(nc.psum_tensor("ps_" + name, shape, dt))

        def atomic(on):
            if Interleave.current is not None:
                Interleave.current.atomic(on)

        def V(fn, r=(), w=()):
            return P.op("vector", fn, r, w)

        def G(fn, r=(), w=()):
            return P.op("gpsimd", fn, r, w)

        def A(fn, r=(), w=()):
            return P.op("scalar", fn, r, w)

        def T(fn, r=(), w=(), sig=True):
            return P.op("tensor", fn, r, w, sig)

        pv = sbt("pv", [128, NPV], F32)
        gfin = sbt("gfin", [128, DM], F32)
        identb = sbt("identb", [128, 128], BF16)
        onesb = sbt("onesb", [128, 128], BF16)
        cst = sbt("cst", [128, 3 * 512 + 64 + 8], F32)
        maskblk = cst[:, 0:512]
        mask64 = cst[:, 512:1024]
        mask8 = cst[:, 1024:1536]
        avals = cst[:, 1536:1600]
        bvals = cst[:, 1600:1608]
        lbv = sbt("lbv", [128, 12], F32)
        Bm = sbt("Bm", [128, 4, 8, 2, 128], BF16)
        Cm = sbt("Cm", [128, 16, 8, 2, 32], BF16)
        T1 = sbt("T1", [128, 2, 16, 64], BF16)
        T2 = sbt("T2", [128, 2, 16, 64], BF16)
        L7 = sbt("L7", [128, 2, 16], F32)
        NWB = 4
        wbuf = [sbt("wbuf%d" % i, [128, 8, 512], BF16) for i in range(NWB)]
        Sf = sbt("Sf", [128, 4, 128], F32)
        SbP = sbt("SbP", [128, 4, 128], BF16)
        Sb8 = sbt("Sb8", [128, 8, 128], BF16)
        LS = sbt("LS", [128, 16, 2, 65], F32)
        muZ = sbt("muZ", [128, 16, 2], F32)
        mzt = sbt("mzt", [128, 4, 16], F32)
        hprev = sbt("hprev", [128, 44, 2], BF16)
        stat = sbt("stat", [128, 16], F32)

        ARENA_KB = 110
        arena = sbt("arena", [128, ARENA_KB * 512], BF16)
        B_ = lambda off, shape, dt: Buf(arena[:], off, shape, dt)
        xt = B_(0, [128, 4, DM], F32)
        uT = B_(16, [128, 8, NT], BF16)
        xs = B_(24, [128, 4, DM], BF16)
        zaT = B_(24, [128, 4, NT], BF16)
        zi = B_(28, [128, 4, NT], BF16)
        qa = B_(32, [128, 4, NT], BF16)
        sf = B_(36, [128, 4, NT], BF16)
        sg = B_(40, [128, 4, NT], BF16)
        Vsb = [B_(88, [128, 4, 2, NT], BF16), B_(96, [128, 4, 2, NT], BF16)]
        mT = B_(32, [128, 8, NT], BF16)
        aT = B_(32, [128, 22, NT], BF16)
        logf = B_(48, [128, NT], F32)
        bcum = B_(50, [128, NT], F32)
        ebts = [B_(52, [128, NT], F32), B_(104, [128, NT], F32)]
        enb = B_(54, [128, NT], F32)
        KhT = B_(56, [128, NT], BF16)
        Khs = [B_(57, [128, 4, 128], BF16), B_(106, [128, 4, 128], BF16)]
        sTs = [B_(58, [128, NT], BF16), B_(107, [128, NT], BF16)]
        osq = B_(59, [128, NT], BF16)
        hrs = B_(60, [128, NT], F32)
        ytm = B_(62, [128, NT], F32)
        l2 = [B_(o_, [128, 4, 64], F32) for o_ in (44, 45, 46, 47, 73, 74)]
        l2x = [B_(o_, [128, 4, 64], F32) for o_ in (108, 109)]
        ybT = B_(65, [128, 4, NT], BF16)
        LSb = B_(69, [128, 4, 2, 64], BF16)
        yv = B_(71, [128, NT], F32)
        y2 = B_(73, [128, NT], F32)
        yaP = B_(75, [128, 4, NT], BF16)
        yaT = B_(79, [128, 4, NT], BF16)
        gsig = B_(83, [128, NT], BF16)
        siga = B_(84, [128, NT], BF16)
        sigb = B_(85, [128, NT], BF16)
        t1 = B_(86, [128, NT], F32)
        hs = [[B_(54 + 4 * p_ + 2 * i, [128, 1024], BF16) for i in range(2)] for p_ in range(2)]
        cc = [[B_(62 + 4 * p_ + 2 * i, [128, NT], F32) for i in range(2)] for p_ in range(2)]
        gact = [B_(70 + p_, [128, NT], BF16) for p_ in range(2)]

        NPB = 8
        pb = [pst("pb%d" % i, [128, 512], F32) for i in range(NPB)]
        pbk = lambda i: "pb%d" % i
        pbh = [pb[i][:].bitcast(BF16) for i in range(NPB)]
        bank_rr = [0]

        held = set()

        def nbank(hold=False):
            tries = 0
            while True:
                b = bank_rr[0]
                bank_rr[0] = (b + 1) % NPB
                if b not in held:
                    break
                tries += 1
                if tries % NPB == 0:
                    assert Interleave.current is not None and tries < 100000, "PSUM banks exhausted"
                    Interleave.current.switch()
            if hold:
                held.add(b)
            return b

        P.dma("sync", pv[:], pv_d, writes=["pv"])
        P.dma("sync", gfin[:], gfin_d, writes=["gfin"])
        P.dma("sync", cst[:], cst_d[:, 128:], writes=["cst"])
        P.dma("gpsimd", identb[:], cst_d[:, 0:128], writes=["identb"])
        V(C("memset", onesb[:], 1.0), w=["onesb"])

        V(C("tensor_tensor", out=lbv[:, 8:12], in0=pv[:, L0:L0 + 4], in1=pv[:, L1:L1 + 4], op=ALU.subtract), r=["pv"], w=["lbv"])
        A(C("activation", out=lbv[:, 0:4], in_=lbv[:, 8:12], func=AF.Sigmoid), r=["lbv"], w=["lbv"])
        A(C("activation", out=lbv[:, 4:8], in_=lbv[:, 8:12], func=AF.Sigmoid, scale=-1.0), r=["lbv"], w=["lbv"])
        V(C("tensor_scalar", out=lbv[:, 8:12], in0=lbv[:, 4:8], scalar1=-1.0, scalar2=None, op0=ALU.mult), r=["lbv"], w=["lbv"])

        prep_id = [0]

        def s5_prep():
            tmp = [None] * 13
            for i_ in range(3, 12):
                tmp[i_] = B_(44 + 4 * (i_ - 3), [128, 1024], F32)
            for n_, i_ in enumerate((0, 1, 2, 12)):
                tmp[i_] = B_(80 + 2 * n_, [128, 512], F32)
            tmi = B_(88, [128, 1024], I32)
            s5s = B_(92, [128, 3, 16], F32)
            s5b = B_(93, [128, 3, 512], F32)
            s5bb = B_(99, [128, 2, 512], F32)
            s5cc = B_(103, [128, 2, 512], F32)
            P.dma("sync", s5s.a, s5s_d, writes=s5s.k())
            P.dma("sync", s5b.a, s5b_d, writes=s5b.k())
            P.dma("sync", s5bb.a, s5bb_d, writes=s5bb.k())
            P.dma("sync", s5cc.a, s5cc_d, writes=s5cc.k())

            def sin_rr(out, xin, n, shift, tA, tB):
                ta, tb = tA.a[:, 0:n], tB.a[:, 0:n]
                ti = tmi.a[:, 0:n]
                rk = out["k"] + xin["k"] + tA.k() + tB.k() + tmi.k()
                V(C("tensor_scalar", out=ta, in0=xin["a"], scalar1=shift, scalar2=None, op0=ALU.add), r=rk, w=tA.k())
                V(C("tensor_scalar", out=ti, in0=ta, scalar1=1.0 / TWO_PI, scalar2=None, op0=ALU.mult), r=rk, w=tmi.k())
                V(C("tensor_copy", out=tb, in_=ti), r=rk, w=tB.k())
                V(C("scalar_tensor_tensor", out=ta, in0=tb, scalar=-TWO_PI_HI, in1=ta, op0=ALU.mult, op1=ALU.add), r=rk, w=tA.k())
                V(C("scalar_tensor_tensor", out=ta, in0=tb, scalar=-TWO_PI_LO, in1=ta, op0=ALU.mult, op1=ALU.add), r=rk, w=tA.k())
                V(C("tensor_scalar", out=tb, in0=ta, scalar1=math.pi, scalar2=None, op0=ALU.is_gt), r=rk, w=tB.k())
                V(C("scalar_tensor_tensor", out=ta, in0=tb, scalar=-TWO_PI_HI, in1=ta, op0=ALU.mult, op1=ALU.add), r=rk, w=tA.k())
                V(C("scalar_tensor_tensor", out=ta, in0=tb, scalar=-TWO_PI_LO, in1=ta, op0=ALU.mult, op1=ALU.add), r=rk, w=tA.k())
                V(C("tensor_scalar", out=tb, in0=ta, scalar1=-math.pi, scalar2=None, op0=ALU.is_lt), r=rk, w=tB.k())
                V(C("scalar_tensor_tensor", out=ta, in0=tb, scalar=TWO_PI_HI, in1=ta, op0=ALU.mult, op1=ALU.add), r=rk, w=tA.k())
                V(C("scalar_tensor_tensor", out=ta, in0=tb, scalar=TWO_PI_LO, in1=ta, op0=ALU.mult, op1=ALU.add), r=rk, w=tA.k())
                V(C("tensor_scalar", out=ta, in0=ta, scalar1=math.pi, scalar2=-math.pi, op0=ALU.min, op1=ALU.max), r=rk, w=tA.k())
                A(C("activation", out=out["a"], in_=ta, func=AF.Sin), r=rk, w=out["k"])

            def cpow(ore, oim, ardt, ang, expo, n, tA, tB, tC, tD):
                xa = dict(a=tC.a[:, 0:n], k=tC.k())
                mg = dict(a=tD.a[:, 0:n], k=tD.k())
                rk = ardt["k"] + ang["k"] + tC.k() + tD.k() + ore["k"] + oim["k"]
                if isinstance(expo, float):
                    V(C("tensor_scalar", out=xa["a"], in0=ang["a"], scalar1=expo, scalar2=None, op0=ALU.mult), r=rk, w=tC.k())
                    A(C("activation", out=mg["a"], in_=ardt["a"], func=AF.Exp, scale=expo), r=rk, w=tD.k())
                else:
                    rk = rk + expo["k"]
                    V(C("tensor_tensor", out=xa["a"], in0=ang["a"], in1=expo["a"], op=ALU.mult), r=rk, w=tC.k())
                    V(C("tensor_tensor", out=mg["a"], in0=ardt["a"], in1=expo["a"], op=ALU.mult), r=rk, w=tD.k())
                    A(C("activation", out=mg["a"], in_=mg["a"], func=AF.Exp), r=rk, w=tD.k())
                sin_rr(oim, xa, n, 0.0, tA, tB)
                sin_rr(ore, xa, n, 0.5 * math.pi, tA, tB)
                V(C("tensor_tensor", out=ore["a"], in0=ore["a"], in1=mg["a"], op=ALU.mult), r=rk, w=ore["k"])
                V(C("tensor_tensor", out=oim["a"], in0=oim["a"], in1=mg["a"], op=ALU.mult), r=rk, w=oim["k"])

            def base(src, n, t_dt, t_ardt, t_ang):
                A(C("activation", out=t_dt.a[:, 0:n], in_=src.a[:, 2], func=AF.Exp), r=src.k(), w=t_dt.k())
                V(C("tensor_tensor", out=t_ardt.a[:, 0:n], in0=src.a[:, 0], in1=t_dt.a[:, 0:n], op=ALU.mult), r=src.k() + t_dt.k(), w=t_ardt.k())
                V(C("tensor_tensor", out=t_ang.a[:, 0:n], in0=src.a[:, 1], in1=t_dt.a[:, 0:n], op=ALU.mult), r=src.k() + t_dt.k(), w=t_ang.k())

            D_ = lambda t, n: dict(a=t.a[:, 0:n], k=t.k())

            n = 512
            t_dt, t_ardt, t_ang = tmp[0], tmp[1], tmp[2]
            base(s5b, n, t_dt, t_ardt, t_ang)
            ardt, ang = D_(t_ardt, n), D_(t_ang, n)
            lr, li = D_(tmp[3], n), D_(tmp[4], n)
            cpow(lr, li, ardt, ang, 1.0, n, tmp[5], tmp[6], tmp[7], tmp[8])
            cor, coi = D_(tmp[9], n), D_(tmp[10], n)
            are, aim = s5b.a[:, 0], s5b.a[:, 1]
            den, nre = tmp[5].a[:, 0:n], tmp[6].a[:, 0:n]
            q1, q2 = tmp[7].a[:, 0:n], tmp[8].a[:, 0:n]
            allk = sum([t.k() for t in tmp], []) + s5b.k() + s5bb.k() + s5cc.k() + s5s.k()
            V(C("tensor_tensor", out=den, in0=are, in1=are, op=ALU.mult), r=allk, w=tmp[5].k())
            V(C("tensor_tensor", out=q1, in0=aim, in1=aim, op=ALU.mult), r=allk, w=tmp[7].k())
            V(C("tensor_tensor", out=den, in0=den, in1=q1, op=ALU.add), r=allk, w=tmp[5].k())
            V(C("reciprocal", out=den, in_=den), r=allk, w=tmp[5].k())
            V(C("tensor_scalar", out=nre, in0=lr["a"], scalar1=-1.0, scalar2=None, op0=ALU.add), r=allk, w=tmp[6].k())
            V(C("tensor_tensor", out=q1, in0=nre, in1=are, op=ALU.mult), r=allk, w=tmp[7].k())
            V(C("tensor_tensor", out=q2, in0=li["a"], in1=aim, op=ALU.mult), r=allk, w=tmp[8].k())
            V(C("tensor_tensor", out=q1, in0=q1, in1=q2, op=ALU.add), r=allk, w=tmp[7].k())
            V(C("tensor_tensor", out=cor["a"], in0=q1, in1=den, op=ALU.mult), r=allk, w=tmp[9].k())
            V(C("tensor_tensor", out=q1, in0=li["a"], in1=are, op=ALU.mult), r=allk, w=tmp[7].k())
            V(C("tensor_tensor", out=q2, in0=nre, in1=aim, op=ALU.mult), r=allk, w=tmp[8].k())
            V(C("tensor_tensor", out=q1, in0=q1, in1=q2, op=ALU.subtract), r=allk, w=tmp[7].k())
            V(C("tensor_tensor", out=coi["a"], in0=q1, in1=den, op=ALU.mult), r=allk, w=tmp[10].k())
            bre = s5bb.a[:, 0]
            bim = s5bb.a[:, 1]
            ir_, ii_ = D_(tmp[3], n), D_(tmp[4], n)
            cpow(ir_, ii_, ardt, ang, -1.0, n, tmp[5], tmp[6], tmp[7], tmp[8])
            pairs = [(tmp[9], tmp[10]), (tmp[11], tmp[12])]
            for b in range(8):
                u1, u2 = tmp[5].a[:, 0:n], tmp[6].a[:, 0:n]
                if b > 0:
                    (pcr, pci), (ncr, nci) = pairs[(b - 1) % 2], pairs[b % 2]
                    V(C("tensor_tensor", out=u1, in0=pcr.a[:, 0:n], in1=ir_["a"], op=ALU.mult), r=allk, w=tmp[5].k())
                    V(C("tensor_tensor", out=u2, in0=pci.a[:, 0:n], in1=ii_["a"], op=ALU.mult), r=allk, w=tmp[6].k())
                    V(C("tensor_tensor", out=ncr.a[:, 0:n], in0=u1, in1=u2, op=ALU.subtract), r=allk, w=ncr.k())
                    V(C("tensor_tensor", out=u1, in0=pcr.a[:, 0:n], in1=ii_["a"], op=ALU.mult), r=allk, w=tmp[5].k())
                    V(C("tensor_tensor", out=u2, in0=pci.a[:, 0:n], in1=ir_["a"], op=ALU.mult), r=allk, w=tmp[6].k())
                    V(C("tensor_tensor", out=nci.a[:, 0:n], in0=u1, in1=u2, op=ALU.add), r=allk, w=nci.k())
                cr, ci = pairs[b % 2][0].a[:, 0:n], pairs[b % 2][1].a[:, 0:n]
                o_re = Bm[:, :, b, 0, :]
                o_im = Bm[:, :, b, 1, :]
                v4 = lambda ap: ap.rearrange("p (a c) -> p a c", c=128)
                V(C("tensor_tensor", out=u1, in0=cr, in1=bre, op=ALU.mult), r=allk, w=tmp[5].k())
                V(C("tensor_tensor", out=u2, in0=ci, in1=bim, op=ALU.mult), r=allk, w=tmp[6].k())
                V(C("tensor_tensor", out=o_re, in0=v4(u1), in1=v4(u2), op=ALU.subtract), r=allk, w=["Bm"])
                V(C("tensor_tensor", out=u1, in0=cr, in1=bim, op=ALU.mult), r=allk, w=tmp[5].k())
                V(C("tensor_tensor", out=u2, in0=ci, in1=bre, op=ALU.mult), r=allk, w=tmp[6].k())
                V(C("tensor_tensor", out=o_im, in0=v4(u1), in1=v4(u2), op=ALU.add), r=allk, w=["Bm"])

            t_dt, t_ardt, t_ang = tmp[0], tmp[1], tmp[2]
            base(s5s, 16, t_dt, t_ardt, t_ang)
            ardt16, ang16 = t_ardt.a[:, 0:16], t_ang.a[:, 0:16]
            n = 128
            ex = D_(tmp[3], n)
            ar_b, an_b = D_(tmp[4], n), D_(tmp[5], n)
            b3 = lambda ap, m: ap.rearrange("p (j a) -> p j a", a=m)
            V(C("tensor_copy", out=b3(ex["a"], 8), in_=bvals.unsqueeze(1).to_broadcast([128, 16, 8])), r=allk + ["cst"], w=tmp[3].k())
            V(C("tensor_copy", out=b3(ar_b["a"], 8), in_=ardt16.unsqueeze(2).to_broadcast([128, 16, 8])), r=allk, w=tmp[4].k())
            V(C("tensor_copy", out=b3(an_b["a"], 8), in_=ang16.unsqueeze(2).to_broadcast([128, 16, 8])), r=allk, w=tmp[5].k())
            pbr, pbi = D_(tmp[6], n), D_(tmp[7], n)
            cpow(pbr, pbi, ar_b, an_b, ex, n, tmp[8], tmp[9], tmp[10], tmp[11])
            V(C("tensor_copy", out=L7[:, 0, :], in_=b3(pbr["a"], 8)[:, :, 7]), r=allk, w=["L7"])
            V(C("tensor_copy", out=L7[:, 1, :], in_=b3(pbi["a"], 8)[:, :, 7]), r=allk, w=["L7"])
            cre = s5cc.a[:, 0].rearrange("p (j c) -> p j c", c=32)
            cim = s5cc.a[:, 1].rearrange("p (j c) -> p j c", c=32)
            w1 = tmp[8].a[:, 0:512].rearrange("p (j c) -> p j c", c=32)
            w2 = tmp[9].a[:, 0:512].rearrange("p (j c) -> p j c", c=32)
            for b in range(8):
                lrb = b3(pbr["a"], 8)[:, :, b:b + 1].to_broadcast([128, 16, 32])
                lib = b3(pbi["a"], 8)[:, :, b:b + 1].to_broadcast([128, 16, 32])
                V(C("tensor_tensor", out=w1, in0=cre, in1=lrb, op=ALU.mult), r=allk, w=tmp[8].k())
                V(C("tensor_tensor", out=w2, in0=cim, in1=lib, op=ALU.mult), r=allk, w=tmp[9].k())
                V(C("tensor_tensor", out=Cm[:, :, b, 0, :], in0=w1, in1=w2, op=ALU.subtract), r=allk, w=["Cm"])
                V(C("tensor_tensor", out=w1, in0=cre, in1=lib, op=ALU.mult), r=allk, w=tmp[8].k())
                V(C("tensor_tensor", out=w2, in0=cim, in1=lrb, op=ALU.mult), r=allk, w=tmp[9].k())
                V(C("scalar_tensor_tensor", out=Cm[:, :, b, 1, :], in0=w1, scalar=-1.0, in1=w2, op0=ALU.mult, op1=ALU.subtract), r=allk, w=["Cm"])
            n = 1024
            ex = D_(tmp[3], n)
            ar_b, an_b = D_(tmp[4], n), D_(tmp[5], n)
            V(C("tensor_copy", out=b3(ex["a"], 64), in_=avals.unsqueeze(1).to_broadcast([128, 16, 64])), r=allk + ["cst"], w=tmp[3].k())
            V(C("tensor_copy", out=b3(ar_b["a"], 64), in_=ardt16.unsqueeze(2).to_broadcast([128, 16, 64])), r=allk, w=tmp[4].k())
            V(C("tensor_copy", out=b3(an_b["a"], 64), in_=ang16.unsqueeze(2).to_broadcast([128, 16, 64])), r=allk, w=tmp[5].k())
            pbr, pbi = D_(tmp[6], n), D_(tmp[7], n)
            cpow(pbr, pbi, ar_b, an_b, ex, n, tmp[8], tmp[9], tmp[10], tmp[11])
            V(C("tensor_copy", out=T2[:, 0], in_=b3(pbr["a"], 64)), r=allk, w=["T2"])
            V(C("tensor_copy", out=T2[:, 1], in_=b3(pbi["a"], 64)), r=allk, w=["T2"])
            V(C("tensor_scalar", out=ex["a"], in0=ex["a"], scalar1=-1.0, scalar2=8.0, op0=ALU.mult, op1=ALU.add), r=allk, w=tmp[3].k())
            cpow(pbr, pbi, ar_b, an_b, ex, n, tmp[8], tmp[9], tmp[10], tmp[11])
            V(C("tensor_copy", out=T1[:, 0], in_=b3(pbr["a"], 64)), r=allk, w=["T1"])
            V(C("tensor_copy", out=T1[:, 1], in_=b3(pbi["a"], 64)), r=allk, w=["T1"])


        wcnt = [0]
        kview = lambda wd, c0, c1, k0, kts: wd[k0 * 128:(k0 + kts) * 128, c0:c1].rearrange("(kt p) c -> p kt c", p=128)
        for g in range(9):
            P.dma("gpsimd", ws_d[g], kview(w_in_d, g * 512, (g + 1) * 512, 0, 8), writes=[("ws", g)])
        for nh in range(2):
            P.dma("gpsimd", ws_d[9 + nh], kview(w_out_d, nh * 512, (nh + 1) * 512, 0, 8), writes=[("ws", 9 + nh)])
        for g in range(11):
            P.dma("gpsimd", ws_d[11 + g][:, :, 0:256], kview(w_up_d, g * 256, (g + 1) * 256, 0, 8), writes=[("ws", 11 + g)])
            P.dma("gpsimd", ws_d[11 + g][:, :, 256:512], kview(w_up_d, DFF + g * 256, DFF + (g + 1) * 256, 0, 8), writes=[("ws", 11 + g)])
        for nh in range(2):
            for ci, (k0, kts) in enumerate(((0, 8), (8, 8), (16, 6))):
                gid = 22 + nh * 3 + ci
                P.dma("gpsimd", ws_d[gid][:, 0:kts, :], kview(w_down_d, nh * 512, (nh + 1) * 512, k0, kts), writes=[("ws", gid)])

        P.dma("gpsimd", ws_d[28][:, 0:4, :], w_glu_d.rearrange("(kt p) c -> p kt c", p=128), writes=[("ws", 28)])
        for gid_, wd_ in ((29, w_pa_d), (30, w_pb_d)):
            for half_ in range(2):
                P.dma("gpsimd", ws_d[gid_].rearrange("p (kt hf) c -> p kt hf c", hf=2)[:, :, half_, :],
                      wd_[:, half_ * 512:(half_ + 1) * 512].rearrange("(kt p) c -> p kt c", p=128), writes=[("ws", gid_)])

        def wfetch(gid, kts=8):
            i = wcnt[0] % NWB
            wcnt[0] += 1
            P.dma("sync", wbuf[i][:, 0:kts, :], ws_d[gid][:, 0:kts, :], reads=[("ws", gid)], writes=[("wb", i)])
            return i

        def norm_tile(tt, xsrc):
            A(C("activation", out=xs.a[:, tt], in_=xsrc.a[:, tt], func=AF.Square, accum_out=stat[:, tt:tt + 1]),
              r=xsrc.k(tt), w=xs.k(tt) + [("stat", tt)])
            A(C("activation", out=stat[:, 4 + tt:5 + tt], in_=stat[:, tt:tt + 1], func=AF.Ln, scale=1.0 / DM, bias=1e-6), r=[("stat", tt)], w=[("stat", tt)])
            A(C("activation", out=stat[:, 8 + tt:9 + tt], in_=stat[:, 4 + tt:5 + tt], func=AF.Exp, scale=-0.5), r=[("stat", tt)], w=[("stat", tt)])
            A(C("activation", out=xs.a[:, tt], in_=xsrc.a[:, tt], func=AF.Copy, scale=stat[:, 8 + tt:9 + tt]),
              r=xsrc.k(tt) + [("stat", tt)], w=xs.k(tt))

        def norm_to_uT(gcol, xsrc, tiles_done=False):
            if not tiles_done:
                for tt in range(4):
                    norm_tile(tt, xsrc)
            for dt in range(8):
                h = nbank()
                for tt in range(4):
                    T(C("transpose", out=pbh[h][:, tt * 128:(tt + 1) * 128],
                                                              in_=xs.a[:, tt, dt * 128:(dt + 1) * 128], identity=identb[:]),
                      r=xs.k(tt) + ["identb"], w=[pbk(h)], sig=(tt == 3))
                if dt % 2 == 0:
                    A(C("activation", out=uT.a[:, dt], in_=pbh[h][:, 0:512], func=AF.Copy, scale=pv[:, gcol + dt:gcol + dt + 1]),
                      r=[pbk(h), "pv"], w=uT.k(dt))
                else:
                    V(C("tensor_scalar", out=uT.a[:, dt], in0=pbh[h][:, 0:512],
                        scalar1=pv[:, gcol + dt:gcol + dt + 1], scalar2=None, op0=ALU.mult),
                      r=[pbk(h), "pv"], w=uT.k(dt))

        def mm_fm(bank, wb_i, mcol, rhs_buf, nk=8):
            for kt in range(nk):
                T(C("matmul", pb[bank][:], lhsT=wbuf[wb_i][:, kt, mcol * 128:(mcol + 1) * 128], rhs=rhs_buf.a[:, kt],
                                            start=(kt == 0), stop=(kt == nk - 1)),
                  r=[("wb", wb_i)] + rhs_buf.k(kt), w=[pbk(bank)], sig=(kt == nk - 1))

        dbg_slot = [0]

        def dbg_out(buf_ap, keys):
            if dbg and dbg_slot[0] < 8:
                P.dma("gpsimd", dbg_d[dbg_slot[0]], buf_ap, reads=keys, final=True)
                dbg_slot[0] += 1

        blocks = [(s_, b_) for s_ in range(nseq) for b_ in range(nblk)]
        xn = B_(72, [128, 4, DM], F32)
        obuf = B_(48, [128, 4, DM], F32)
        stat2 = sbt("stat2", [128, 16], F32)

        def prefetch_x(idx):
            s_, b_ = blocks[idx]
            for tt in range(4):
                P.dma("sync", xn.a[:, tt], x_d[s_, b_ * NT + tt * 128:b_ * NT + (tt + 1) * 128, :], writes=xn.k(tt))

        def xn_to_xt():
            for tt in range(4):
                P.op("gpsimd", C("tensor_copy", out=xt.a[:, tt], in_=xn.a[:, tt]), xn.k(tt), xt.k(tt))

        def inproj_block():
            wi = wfetch(0)
            for m in range(4):
                b = nbank()
                mm_fm(b, wi, m, uT)
                A(C("activation", out=zaT.a[:, m], in_=pb[b][:], func=AF.Copy), r=[pbk(b)], w=zaT.k(m))
            wi = wfetch(1)
            for m in range(4):
                b = nbank()
                mm_fm(b, wi, m, uT)
                A(C("activation", out=qa.a[:, m], in_=pb[b][:], func=AF.Silu), r=[pbk(b)], w=qa.k(m))
            wi = wfetch(2)
            for m in range(4):
                b = nbank()
                mm_fm(b, wi, m, uT)
                A(C("activation", out=sf.a[:, m], in_=pb[b][:], func=AF.Sigmoid), r=[pbk(b)], w=sf.k(m))
            wi = wfetch(3)
            for tt in range(4):
                b = nbank()
                for kt in range(8):
                    T(C("matmul", pb[b][:], lhsT=uT.a[:, kt, tt * 128:(tt + 1) * 128], rhs=wbuf[wi][:, kt, :],
                                                                     start=(kt == 0), stop=(kt == 7)),
                      r=[("wb", wi)] + uT.k(kt), w=[pbk(b)], sig=(kt == 7))
                V(C("tensor_copy", out=zi.a[:, tt], in_=pb[b][:]), r=[pbk(b)], w=zi.k(tt))
            wi = wfetch(4)
            for m in range(4):
                b = nbank()
                mm_fm(b, wi, m, uT)
                A(C("activation", out=sg.a[:, m], in_=pb[b][:], func=AF.Silu), r=[pbk(b)], w=sg.k(m))


        for tt in range(4):
            P.dma("sync", xt.a[:, tt], x_d[blocks[0][0], blocks[0][1] * NT + tt * 128:blocks[0][1] * NT + (tt + 1) * 128, :], writes=xt.k(tt))
        norm_to_uT(GM, xt)
        inproj_block()
        s5_prep()
        for idx, (s, bi) in enumerate(blocks):
            if True:
                t0 = bi * NT
                first = bi == 0
                more = idx + 1 < len(blocks)
                if first:
                    V(C("memset", Sf[:], 0.0), w=[("Sf", h_) for h_ in range(4)])
                    V(C("memset", SbP[:], 0.0), w=[("SbP", h_) for h_ in range(4)])
                    V(C("memset", LS[:], 0.0), w=[(k_, i_) for i_ in range(4) for k_ in ("LSr", "LSi", "LSc")])
                    V(C("memset", muZ[:], 0.0), w=[("muZ", i_) for i_ in range(4)])
                    V(C("memset", hprev[:], 0.0), w=[("hprev", c_) for c_ in range(44)])

                if idx > 0:
                    inproj_block()

                hst = {}

                def h_A(h):
                    ebt, Kh, sT = ebts[h % 2], Khs[h % 2], sTs[h % 2]
                    lbc = lbv[:, h:h + 1]
                    omc = lbv[:, 4 + h:5 + h]
                    nomc = lbv[:, 8 + h:9 + h]
                    A(C("activation", out=logf.a, in_=sf.a[:, h], func=AF.Ln, scale=omc, bias=lbc),
                      r=sf.k(h) + ["lbv"], w=logf.k())
                    V(C("tensor_scalar", out=sf.a[:, h], in0=sf.a[:, h], scalar1=nomc, scalar2=omc, op0=ALU.mult, op1=ALU.add),
                      r=sf.k(h) + ["lbv"], w=sf.k(h))
                    V(C("tensor_tensor_scan", out=bcum.a, data0=mask64, data1=logf.a, initial=0.0, op0=ALU.mult, op1=ALU.add),
                      r=logf.k() + ["cst"], w=bcum.k())
                    A(C("activation", out=ebt.a, in_=bcum.a, func=AF.Exp), r=bcum.k(), w=ebt.k())
                    A(C("activation", out=enb.a, in_=bcum.a, func=AF.Exp, scale=-1.0), r=bcum.k(), w=enb.k())
                    V(C("scalar_tensor_tensor", out=qa.a[:, h], in0=qa.a[:, h], scalar=128.0 ** -0.5, in1=ebt.a, op0=ALU.mult, op1=ALU.mult),
                      r=qa.k(h) + ebt.k(), w=qa.k(h))
                    V(C("tensor_tensor", out=sf.a[:, h], in0=sf.a[:, h], in1=enb.a, op=ALU.mult), r=sf.k(h) + enb.k(), w=sf.k(h))
                    eb3 = ebt.a.rearrange("p (c s) -> p c s", s=64)
                    V(C("tensor_tensor", out=KhT.a.rearrange("p (c s) -> p c s", s=64), in0=sf.a[:, h].rearrange("p (c s) -> p c s", s=64),
                        in1=eb3[:, :, 63:64].to_broadcast([128, 8, 64]), op=ALU.mult),
                      r=sf.k(h) + ebt.k(), w=KhT.k())
                    bt = nbank(hold=True)
                    for tt in range(4):
                        T(C("transpose", out=pbh[bt][:, tt * 128:(tt + 1) * 128], in_=KhT.a[:, tt * 128:(tt + 1) * 128], identity=identb[:]),
                          r=KhT.k() + ["identb"], w=[pbk(bt)], sig=(tt == 3))
                    A(C("activation", out=Kh.a, in_=pbh[bt][:, 0:512].rearrange("p (a b) -> p a b", b=128), func=AF.Copy), r=[pbk(bt)], w=Kh.k())
                    held.discard(bt)
                    bs = nbank(hold=True)
                    for tt in range(4):
                        T(C("matmul", pb[bs][:, tt * 128:(tt + 1) * 128], lhsT=sf.a[:, h, tt * 128:(tt + 1) * 128],
                            rhs=qa.a[:, h, tt * 128:(tt + 1) * 128], start=True, stop=True),
                          r=sf.k(h) + qa.k(h), w=[pbk(bs)], sig=(tt == 3))
                    V(C("tensor_tensor", out=sT.a, in0=pb[bs][:], in1=maskblk, op=ALU.mult), r=[pbk(bs), "cst"], w=sT.k())
                    held.discard(bs)

                def h_B(h):
                    ebt, Kh, sT = ebts[h % 2], Khs[h % 2], sTs[h % 2]
                    bo = nbank(hold=True)
                    hst["bo"] = bo
                    kvb = {}

                    def kv(c):
                        tt, hf = c // 2, c % 2
                        bk_ = nbank(hold=True)
                        kvb[c] = bk_
                        T(C("matmul", pb[bk_][:, 0:128], lhsT=Kh.a[hf * 64:(hf + 1) * 64, tt, :],
                            rhs=zi.a[hf * 64:(hf + 1) * 64, tt, h * 128:(h + 1) * 128], start=True, stop=True),
                          r=Kh.k() + zi.k(tt), w=[pbk(bk_)])

                    kv(0)
                    for c in range(8):
                        tt, hf = c // 2, c % 2
                        sprev = SbP[:, h, :] if c == 0 else Sb8[:, c, :]
                        skey = ("SbP", h) if c == 0 else ("Sb8", c)
                        T(C("matmul", pb[bo][:, c * 64:(c + 1) * 64], lhsT=sprev, rhs=qa.a[:, h, c * 64:(c + 1) * 64],
                            start=(c == 0), stop=False),
                          r=[skey] + qa.k(h), w=[pbk(bo)], sig=False)
                        if hf == 1:
                            T(C("matmul", pb[bo][:, tt * 128:(tt + 1) * 128], lhsT=zi.a[:, tt, h * 128:(h + 1) * 128],
                                rhs=sT.a[:, tt * 128:(tt + 1) * 128], start=False, stop=(tt == 3)),
                              r=zi.k(tt) + sT.k(), w=[pbk(bo)], sig=(tt == 3))
                        if c + 1 < 8:
                            kv(c + 1)
                        bk_ = kvb[c]
                        ebl = ebt.a[:, c * 64 + 63:c * 64 + 64]
                        V(C("scalar_tensor_tensor", out=Sf[:, h, :], in0=Sf[:, h, :], scalar=ebl, in1=pb[bk_][:, 0:128],
                            op0=ALU.mult, op1=ALU.add),
                          r=[("Sf", h), pbk(bk_)] + ebt.k(), w=[("Sf", h)])
                        held.discard(bk_)
                        if c < 7:
                            A(C("activation", out=Sb8[:, c + 1, :], in_=Sf[:, h, :], func=AF.Copy), r=[("Sf", h)], w=[("Sb8", c + 1)])
                        else:
                            A(C("activation", out=SbP[:, h, :], in_=Sf[:, h, :], func=AF.Copy), r=[("Sf", h)], w=[("SbP", h)])

                def h_C(h):
                    bo = hst["bo"]
                    A(C("activation", out=osq.a, in_=pb[bo][:], func=AF.Square), r=[pbk(bo)], w=osq.k())
                    bm_ = nbank(hold=True)
                    T(C("matmul", pb[bm_][:], lhsT=onesb[:], rhs=osq.a, start=True, stop=True), r=["onesb"] + osq.k(), w=[pbk(bm_)])
                    A(C("activation", out=hrs.a, in_=pb[bm_][:], func=AF.Ln, scale=1.0 / 128, bias=1e-6), r=[pbk(bm_)], w=hrs.k())
                    held.discard(bm_)
                    A(C("activation", out=hrs.a, in_=hrs.a, func=AF.Exp, scale=-0.5), r=hrs.k(), w=hrs.k())
                    V(C("scalar_tensor_tensor", out=ytm.a, in0=pb[bo][:], scalar=pv[:, HG + h:HG + h + 1], in1=hrs.a, op0=ALU.mult, op1=ALU.mult),
                      r=[pbk(bo), "pv"] + hrs.k(), w=ytm.k())
                    V(C("tensor_tensor", out=ybT.a[:, h], in0=ytm.a, in1=sg.a[:, h], op=ALU.mult), r=ytm.k() + sg.k(h), w=ybT.k(h))
                    held.discard(bo)

                def s5_A(i):
                    Vs = Vsb[i % 2]
                    jz = i
                    za3 = zaT.a[:, jz].rearrange("p (a b) -> p a b", b=8)
                    for pl in range(2):
                        for j0 in (0, 2):
                            bks = {jl: nbank(hold=True) for jl in (j0, j0 + 1)}
                            atomic(True)
                            for b in range(8):
                                for jl in (j0, j0 + 1):
                                    q = jl
                                    T(C("matmul", pb[bks[jl]][:, b * 64:512], lhsT=Bm[32 * q:32 * q + 32, jz, b, pl, :],
                                        rhs=za3[32 * q:32 * q + 32, :, b].unsqueeze(1).to_broadcast([32, 8 - b, 64]),
                                        start=(b == 0), stop=(b == 7), tile_position=(32 * q, 0)),
                                      r=["Bm"] + zaT.k(jz), w=[pbk(bks[jl])], sig=(b == 7))
                            atomic(False)
                            for jl in (j0, j0 + 1):
                                A(C("activation", out=Vs.a[:, jl, pl], in_=pb[bks[jl]][:], func=AF.Copy), r=[pbk(bks[jl])], w=Vs.k(jl))
                                held.discard(bks[jl])

                def s5_B(i):
                    Vs = Vsb[i % 2]
                    js = slice(4 * i, 4 * i + 4)
                    Ere = Vs.a[:, :, 0, 448:512]
                    Eim = Vs.a[:, :, 1, 448:512]
                    t1r, t1i = T1[:, 0, js, :], T1[:, 1, js, :]
                    t2r, t2i = T2[:, 0, js, :], T2[:, 1, js, :]
                    Bmre, Bmim, Bu1, Bu2, Bpre, Bpim = l2
                    Bu3, Bu4 = l2x
                    mre, mim, u1, u2, pre, pim = [b.a for b in l2]
                    u3, u4 = Bu3.a, Bu4.a
                    vk = Vs.k()
                    KR, KI, KC, KM = [("LSr", i)], [("LSi", i)], [("LSc", i)], [("muZ", i)]
                    G(C("tensor_tensor", out=u1, in0=t1r, in1=Ere, op=ALU.mult), r=vk + ["T1"], w=Bu1.k())
                    V(C("tensor_tensor", out=u3, in0=t1r, in1=Eim, op=ALU.mult), r=vk + ["T1"], w=Bu3.k())
                    G(C("tensor_tensor", out=u2, in0=t1i, in1=Eim, op=ALU.mult), r=vk + ["T1"], w=Bu2.k())
                    V(C("tensor_tensor", out=u4, in0=t1i, in1=Ere, op=ALU.mult), r=vk + ["T1"], w=Bu4.k())
                    G(C("tensor_tensor", out=mre, in0=u1, in1=u2, op=ALU.subtract), r=Bu1.k() + Bu2.k(), w=Bmre.k())
                    V(C("tensor_tensor", out=mim, in0=u3, in1=u4, op=ALU.add), r=Bu3.k() + Bu4.k(), w=Bmim.k())
                    fl = lambda ap: ap.rearrange("p a b -> p (a b)")
                    V(C("tensor_tensor_scan", out=fl(pim), data0=mask64[:, 0:256], data1=fl(mim), initial=0.0, op0=ALU.mult, op1=ALU.add), r=Bmim.k() + ["cst"], w=Bpim.k())
                    V(C("tensor_tensor_scan", out=fl(pre), data0=mask64[:, 0:256], data1=fl(mre), initial=0.0, op0=ALU.mult, op1=ALU.add), r=Bmre.k() + ["cst"], w=Bpre.k())
                    G(C("tensor_tensor", out=pre, in0=pre, in1=muZ[:, js, 0:1].to_broadcast([128, 4, 64]), op=ALU.add), r=Bpre.k() + KM, w=Bpre.k())
                    V(C("tensor_tensor", out=pim, in0=pim, in1=muZ[:, js, 1:2].to_broadcast([128, 4, 64]), op=ALU.add), r=Bpim.k() + KM, w=Bpim.k())
                    lsr = LS[:, js, 0, 1:65]
                    lsi = LS[:, js, 1, 1:65]
                    G(C("tensor_tensor", out=u1, in0=t2r, in1=pre, op=ALU.mult), r=Bpre.k() + ["T2"], w=Bu1.k())
                    V(C("tensor_tensor", out=u3, in0=t2r, in1=pim, op=ALU.mult), r=Bpim.k() + ["T2"], w=Bu3.k())
                    G(C("tensor_tensor", out=u2, in0=t2i, in1=pim, op=ALU.mult), r=Bpim.k() + ["T2"], w=Bu2.k())
                    V(C("tensor_tensor", out=u4, in0=t2i, in1=pre, op=ALU.mult), r=Bpre.k() + ["T2"], w=Bu4.k())
                    G(C("tensor_tensor", out=lsr, in0=u1, in1=u2, op=ALU.subtract), r=Bu1.k() + Bu2.k(), w=KR)
                    V(C("tensor_tensor", out=lsi, in0=u3, in1=u4, op=ALU.add), r=Bu3.k() + Bu4.k(), w=KI)
                    V(C("tensor_copy", out=LSb.a, in_=LS[:, js, :, 0:64]), r=KR + KI + KC, w=LSb.k())
                    G(C("tensor_copy", out=LS[:, js, :, 0:1], in_=LS[:, js, :, 64:65]), r=KR + KI + LSb.k(), w=KC)
                    zr = LS[:, js, 0, 0]
                    zim = LS[:, js, 1, 0]
                    l7r, l7i = L7[:, 0, js], L7[:, 1, js]
                    s_a, s_b, s_c, s_d = mzt[:, 0, js], mzt[:, 1, js], mzt[:, 2, js], mzt[:, 3, js]
                    KT = [("mzt", i)]
                    G(C("tensor_tensor", out=s_a, in0=l7r, in1=zr, op=ALU.mult), r=KC + ["L7"], w=KT)
                    G(C("tensor_tensor", out=s_b, in0=l7i, in1=zim, op=ALU.mult), r=KC + ["L7"], w=KT)
                    G(C("tensor_tensor", out=muZ[:, js, 0], in0=s_a, in1=s_b, op=ALU.subtract), r=KT, w=KM)
                    G(C("tensor_tensor", out=s_c, in0=l7r, in1=zim, op=ALU.mult), r=KC + ["L7"], w=KT)
                    G(C("tensor_tensor", out=s_d, in0=l7i, in1=zr, op=ALU.mult), r=KC + ["L7"], w=KT)
                    G(C("tensor_tensor", out=muZ[:, js, 1], in0=s_c, in1=s_d, op=ALU.add), r=KT, w=KM)

                def s5_C(i):
                    Vs = Vsb[i % 2]
                    jz = i
                    by = nbank(hold=True)
                    cnt = 0
                    atomic(True)
                    for b in range(8):
                        for pl in range(2):
                            for src in range(2):
                                for q in range(4):
                                    jl = q
                                    j = 4 * i + jl
                                    rhs = Vs.a[:, jl, pl, b * 64:(b + 1) * 64] if src == 0 else LSb.a[:, jl, pl, :]
                                    T(C("matmul", pb[by][32 * q:32 * q + 32, b::8], lhsT=Cm[:, j, b, pl, :], rhs=rhs,
                                        start=(cnt == 0), stop=(cnt == 31), tile_position=(0, 32 * q)),
                                      r=["Cm"] + Vs.k(jl) + LSb.k(), w=[pbk(by)], sig=(cnt == 31 and q == 3))
                                cnt += 1
                    atomic(False)
                    V(C("scalar_tensor_tensor", out=yv.a, in0=zaT.a[:, jz], scalar=pv[:, SD + jz:SD + jz + 1], in1=pb[by][:], op0=ALU.mult, op1=ALU.add),
                      r=zaT.k(jz) + [pbk(by), "pv"], w=yv.k())
                    held.discard(by)
                    A(C("activation", out=yaP.a[:, jz], in_=yv.a, func=AF.Gelu_apprx_tanh), r=yv.k(), w=yaP.k(jz))

                Interleave([lambda: s5_A(0), lambda: h_A(0)]).run()
                for i in range(4):
                    chains = [lambda i=i: (s5_B(i), s5_C(i)), lambda i=i: (h_B(i), h_C(i))]
                    if i < 3:
                        chains += [lambda i=i: s5_A(i + 1), lambda i=i: h_A(i + 1)]
                    Interleave(chains).run()
                wgl = wfetch(28, kts=4)
                for ct in range(4):
                    b = nbank()
                    for kt in range(4):
                        T(C("matmul", pb[b][:], lhsT=wbuf[wgl][:, kt, ct * 128:(ct + 1) * 128], rhs=yaP.a[:, kt], start=(kt == 0), stop=(kt == 3)),
                          r=[("wb", wgl)] + yaP.k(kt), w=[pbk(b)], sig=(kt == 3))
                    A(C("activation", out=gsig.a, in_=pb[b][:], func=AF.Sigmoid, bias=pv[:, BG + ct:BG + ct + 1]), r=[pbk(b), "pv"], w=gsig.k())
                    V(C("tensor_tensor", out=yaT.a[:, ct], in0=yaP.a[:, ct], in1=gsig.a, op=ALU.mult), r=yaP.k(ct) + gsig.k(), w=yaT.k(ct))

                if dbg == 1:
                    for i_ in range(4):
                        dbg_out(yaT.a[:, i_], yaT.k(i_))
                    for i_ in range(4):
                        dbg_out(ybT.a[:, i_], ybT.k(i_))
                if dbg == 2:
                    for i_ in range(4):
                        dbg_out(yaP.a[:, i_], yaP.k(i_))
                    for i_ in range(4):
                        dbg_out(zaT.a[:, i_], zaT.k(i_))
                for pas, (gw, gg, ysrc, sbuf_) in enumerate(((29, 5, yaT, siga), (30, 7, ybT, sigb))):
                    wg0_ = wfetch(gg)
                    wp_ = wfetch(gw)
                    for half in range(2):
                        wg_ = wg0_ if half == 0 else wfetch(gg + half)
                        for m in range(4):
                            ct = half * 4 + m
                            b1 = nbank()
                            mm_fm(b1, wg_, m, uT)
                            A(C("activation", out=sbuf_.a, in_=pb[b1][:], func=AF.Sigmoid), r=[pbk(b1)], w=sbuf_.k())
                            b2 = nbank()
                            for kt in range(4):
                                T(C("matmul", pb[b2][:], lhsT=wbuf[wp_][:, 2 * kt + half, m * 128:(m + 1) * 128], rhs=ysrc.a[:, kt], start=(kt == 0), stop=(kt == 3)),
                                  r=[("wb", wp_)] + ysrc.k(kt), w=[pbk(b2)], sig=(kt == 3))
                            if pas == 0:
                                V(C("tensor_tensor", out=mT.a[:, ct], in0=pb[b2][:], in1=sbuf_.a, op=ALU.mult), r=[pbk(b2)] + sbuf_.k(), w=mT.k(ct))
                            else:
                                V(C("tensor_tensor", out=t1.a, in0=pb[b2][:], in1=sbuf_.a, op=ALU.mult), r=[pbk(b2)] + sbuf_.k(), w=t1.k())
                                V(C("tensor_tensor", out=mT.a[:, ct], in0=t1.a, in1=mT.a[:, ct], op=ALU.add), r=t1.k() + mT.k(ct), w=mT.k(ct))

                wos = [wfetch(9), wfetch(10)]
                for tt in range(4):
                    for nh in range(2):
                        wo = wos[nh]
                        b = nbank()
                        for kt in range(8):
                            T(C("matmul", pb[b][:], lhsT=mT.a[:, kt, tt * 128:(tt + 1) * 128], rhs=wbuf[wo][:, kt, :], start=(kt == 0), stop=(kt == 7)),
                              r=[("wb", wo)] + mT.k(kt), w=[pbk(b)], sig=(kt == 7))
                        V(C("tensor_tensor", out=xt.a[:, tt, nh * 512:(nh + 1) * 512], in0=pb[b][:], in1=xt.a[:, tt, nh * 512:(nh + 1) * 512], op=ALU.add),
                          r=[pbk(b)] + xt.k(tt), w=xt.k(tt))
                    norm_tile(tt, xt)

                norm_to_uT(GF, xt, tiles_done=True)
                if more:
                    prefetch_x(idx + 1)
                for g in range(11):
                    wu = wfetch(11 + g)
                    for pi in range(2):
                        m = 2 * g + pi
                        par = m % 2
                        for gv in range(2):
                            col = m + 22 * gv
                            b = nbank()
                            mm_fm(b, wu, gv * 2 + pi, uT)
                            hb = hs[par][gv]
                            c0 = cc[par][gv]
                            P.op("gpsimd", C("tensor_copy", out=hb.a[:, 0:2], in_=hprev[:, col, :]), [("hprev", col)], hb.k())
                            A(C("activation", out=hb.a[:, 2:514], in_=pb[b][:], func=AF.Copy), r=[pbk(b)], w=hb.k())
                            A(C("activation", out=c0.a, in_=pb[b][:], func=AF.Identity, scale=pv[:, WC2 + col:WC2 + col + 1], bias=pv[:, BC + col:BC + col + 1]),
                              r=[pbk(b), "pv"], w=c0.k())
                            V(C("scalar_tensor_tensor", out=c0.a, in0=hb.a[:, 1:513], scalar=pv[:, WC1 + col:WC1 + col + 1], in1=c0.a, op0=ALU.mult, op1=ALU.add),
                              r=hb.k() + c0.k() + ["pv"], w=c0.k())
                            V(C("scalar_tensor_tensor", out=c0.a, in0=hb.a[:, 0:512], scalar=pv[:, WC0 + col:WC0 + col + 1], in1=c0.a, op0=ALU.mult, op1=ALU.add),
                              r=hb.k() + c0.k() + ["pv"], w=c0.k())
                            P.op("gpsimd", C("tensor_copy", out=hprev[:, col, :], in_=hb.a[:, 512:514]), hb.k(), [("hprev", col)])
                        A(C("activation", out=gact[par].a, in_=cc[par][0].a, func=AF.Silu), r=cc[par][0].k(), w=gact[par].k())
                        V(C("tensor_tensor", out=aT.a[:, m], in0=gact[par].a, in1=cc[par][1].a, op=ALU.mult), r=gact[par].k() + cc[par][1].k(), w=aT.k(m))
                if more:
                    for tt_ in range(4):
                        norm_tile(tt_, xn)
                for nh in range(2):
                    banks = [nbank(hold=True) for _ in range(4)]
                    for ci, (k0, kts) in enumerate(((0, 8), (8, 8), (16, 6))):
                        wd = wfetch(22 + nh * 3 + ci, kts=kts)
                        for tt in range(4):
                            for kt in range(kts):
                                kk = k0 + kt
                                T(C("matmul", pb[banks[tt]][:], lhsT=aT.a[:, kk, tt * 128:(tt + 1) * 128], rhs=wbuf[wd][:, kt, :],
                                                                                   start=(kk == 0), stop=(kk == 21)),
                                  r=[("wb", wd)] + aT.k(kk), w=[pbk(banks[tt])], sig=(kt == kts - 1))
                    for tt in range(4):
                        b = banks[tt]
                        V(C("tensor_tensor", out=xt.a[:, tt, nh * 512:(nh + 1) * 512], in0=pb[b][:], in1=xt.a[:, tt, nh * 512:(nh + 1) * 512], op=ALU.add),
                          r=[pbk(b)] + xt.k(tt), w=xt.k(tt))
                        held.discard(b)
                    if nh == 0 and more:
                        norm_to_uT(GM, xn, tiles_done=True)
                for tt in range(4):
                    A(C("activation", out=obuf.a[:, tt], in_=xt.a[:, tt], func=AF.Square, accum_out=stat2[:, tt:tt + 1]), r=xt.k(tt), w=obuf.k(tt) + ["stat2"])
                A(C("activation", out=stat2[:, 4:8], in_=stat2[:, 0:4], func=AF.Ln, scale=1.0 / DM, bias=1e-6), r=["stat2"], w=["stat2"])
                A(C("activation", out=stat2[:, 8:12], in_=stat2[:, 4:8], func=AF.Exp, scale=-0.5), r=["stat2"], w=["stat2"])
                for tt in range(4):
                    V(C("scalar_tensor_tensor", out=obuf.a[:, tt], in0=xt.a[:, tt], scalar=stat2[:, 8 + tt:9 + tt], in1=gfin[:], op0=ALU.mult, op1=ALU.mult),
                      r=xt.k(tt) + ["stat2", "gfin"], w=obuf.k(tt))
                    P.dma("gpsimd", out_d[s, t0 + tt * 128:t0 + (tt + 1) * 128, :], obuf.a[:, tt], reads=obuf.k(tt), final=True)
                if more:
                    xn_to_xt()
        P.finalize()
    return nc


def _pvec(v):
    v = np.asarray(v, np.float32).reshape(-1)
    return np.ascontiguousarray(v.reshape(-1, 128).T)


def host_layout(inp):
    f = lambda k: np.asarray(inp[k], np.float32)
    pvcols = [_pvec(f("g_mix")[0]), _pvec(f("g_ffn")[0]), _pvec(f("b_glu")[0]), _pvec(f("s5_d")[0]), _pvec(f("hg_norm_gain")[0]),
              _pvec(f("hg_lb_logits")[0]), _pvec(f("hg_lb_logits")[1]), _pvec(f("w_conv")[0, 0]), _pvec(f("w_conv")[0, 1]),
              _pvec(f("w_conv")[0, 2]), _pvec(f("b_conv")[0])]
    pv = np.ascontiguousarray(np.concatenate(pvcols, axis=1))
    assert pv.shape == (128, NPV)
    gfin = np.ascontiguousarray(np.broadcast_to(f("g_final")[None, :], (128, DM)))
    a_re, a_im, ldt = f("s5_a_re")[0], f("s5_a_im")[0], f("s5_log_dt")[0]
    b_re, b_im = f("s5_b_re")[0], f("s5_b_im")[0]
    c_re, c_im = f("s5_c_re")[0], f("s5_c_im")[0]
    s5s = np.zeros((128, 3, 16), np.float32)
    for j in range(16):
        for g2 in range(2):
            g = 2 * j + g2
            s5s[g2 * 64:(g2 + 1) * 64, 0, j] = a_re[g]
            s5s[g2 * 64:(g2 + 1) * 64, 1, j] = a_im[g]
            s5s[g2 * 64:(g2 + 1) * 64, 2, j] = ldt[g]
    s5b = np.zeros((128, 3, 4, 2, 64), np.float32)
    s5bb = np.zeros((128, 2, 4, 2, 64), np.float32)
    for q in range(4):
        for jz in range(4):
            for g2 in range(2):
                g = 2 * (4 * jz + q) + g2
                s5b[32 * q:32 * q + 32, 0, jz, g2, :] = a_re[g][None, :]
                s5b[32 * q:32 * q + 32, 1, jz, g2, :] = a_im[g][None, :]
                s5b[32 * q:32 * q + 32, 2, jz, g2, :] = ldt[g]
                r0 = 32 * q + 16 * g2
                s5bb[r0:r0 + 16, 0, jz, g2, :] = b_re[g].T
                s5bb[r0:r0 + 16, 1, jz, g2, :] = b_im[g].T
    s5cc = np.zeros((128, 2, 16, 2, 16), np.float32)
    for j in range(16):
        for g2 in range(2):
            g = 2 * j + g2
            s5cc[g2 * 64:(g2 + 1) * 64, 0, j, g2, :] = c_re[g].T
            s5cc[g2 * 64:(g2 + 1) * 64, 1, j, g2, :] = c_im[g].T
    cst = np.zeros((128, 128 + 3 * 512 + 64 + 8), np.float32)
    cst[:, 0:128] = np.eye(128, dtype=np.float32)
    sidx = np.arange(128)[:, None]
    tidx = np.arange(128)[None, :]
    blk = ((sidx // 64) == (tidx // 64)) & (sidx <= tidx)
    cst[:, 128:640] = np.tile(blk.astype(np.float32), (1, 4))
    m64 = np.ones(512, np.float32)
    m64[::64] = 0
    m8 = np.ones(512, np.float32)
    m8[::8] = 0
    cst[:, 640:1152] = m64[None, :]
    cst[:, 1152:1664] = m8[None, :]
    cst[:, 1664:1728] = (8.0 * np.arange(64) + 1.0)[None, :]
    cst[:, 1728:1736] = np.arange(8, dtype=np.float32)[None, :]
    shared = dict(
        w_in=np.ascontiguousarray(f("w_in")[0]), w_glu=np.ascontiguousarray(f("w_glu")[0]), w_pa=np.ascontiguousarray(f("w_pa")[0]),
        w_pb=np.ascontiguousarray(f("w_pb")[0]), w_out=np.ascontiguousarray(f("w_out")[0]), w_up=np.ascontiguousarray(f("w_up")[0]),
        w_down=np.ascontiguousarray(f("w_down")[0]), pv=pv, gfin=gfin, s5s=s5s,
        s5b=np.ascontiguousarray(s5b.reshape(128, 3, 512)), s5bb=np.ascontiguousarray(s5bb.reshape(128, 2, 512)),
        s5cc=np.ascontiguousarray(s5cc.reshape(128, 2, 512)), cst=cst)
    return shared


_NC_CACHE = {}


def kernel(**inputs):
    x = np.asarray(inputs["x"], np.float32)
    shared = host_layout(inputs)
    if "nc" not in _NC_CACHE:
        _NC_CACHE["nc"] = build_program()
    nc = _NC_CACHE["nc"]
    in_maps = []
    for c in range(NCORES):
        m = dict(shared)
        m["x"] = np.ascontiguousarray(x[2 * c:2 * c + 2])
        in_maps.append(m)
    res = run_bass_kernel_spmd(nc, in_maps, core_ids=list(range(NCORES)))
    out = np.concatenate([np.asarray(r["out"], np.float32) for r in res.results], axis=0)
    return out
```
